# Optimizing a Trainium2 kernel written in Bass

```python
import jax
import jax.numpy as jnp
from jax import lax
import numpy as np

D_MODEL = 2048
BATCH = 4
SEQ = 4096
DEPTH = 2

GRID_W = 64
CTX_LEN = 256
N_BRANCH = 4
BRANCH_W = 512
MLA_HEADS = 4
QK_NOPE = 128
QK_ROPE = 64
V_HEAD = 128
Q_LORA = 512
KV_LORA = 256
KV_COLS = KV_LORA + QK_ROPE
ROPE_THETA = 10000.0
ATTN_BLOCK = 128
FOURIER_GROUPS = 4
CONV_W = 3
CHUNK = 128
SGU_GROUPS = 4
IN_WIDTH = KV_LORA + QK_ROPE + Q_LORA + 10 * BRANCH_W + N_BRANCH * D_MODEL
EPS = 1e-6

kernel_name = 'hybrid_parallel_mla_fnet_conv_gmlp_dit'


def rmsnorm(x, g):
    xf = x.astype(jnp.float32)
    y = xf * lax.rsqrt(jnp.mean(xf * xf, axis=-1, keepdims=True) + EPS)
    return (y * g.astype(jnp.float32)).astype(x.dtype)


def layernorm(x, g, b):
    xf = x.astype(jnp.float32)
    xc = xf - jnp.mean(xf, axis=-1, keepdims=True)
    y = xc * lax.rsqrt(jnp.mean(xc * xc, axis=-1, keepdims=True) + EPS)
    return (y * g.astype(jnp.float32) + b.astype(jnp.float32)).astype(x.dtype)


def modulate(h, shift, scale):
    return h * (1 + scale[:, None, :]) + shift[:, None, :]


def axial_rope(n, dtype):
    rows = n // GRID_W
    r, col = jnp.meshgrid(jnp.arange(rows, dtype=jnp.float32),
                          jnp.arange(GRID_W, dtype=jnp.float32), indexing='ij')
    half = QK_ROPE // 2
    inv = ROPE_THETA ** (-jnp.arange(0, half, 2, dtype=jnp.float32) / half)
    ang = jnp.concatenate([r.reshape(-1, 1) * inv, col.reshape(-1, 1) * inv], axis=-1)
    return jnp.cos(ang).astype(dtype), jnp.sin(ang).astype(dtype)


def apply_rope(x, cos, sin):
    x1, x2 = jnp.split(x, 2, axis=-1)
    return jnp.concatenate([x1 * cos - x2 * sin, x1 * sin + x2 * cos], axis=-1)


def split_proj(p):
    sizes = (KV_LORA, QK_ROPE, Q_LORA) + (BRANCH_W,) * 10
    idx = np.cumsum(sizes).tolist()
    parts = jnp.split(p, idx, axis=-1)
    names = ('ckv', 'krope', 'cq', 'zA', 'uB', 'zB', 'xC', 'bC', 'cC', 'zC', 'uD', 'vD', 'zD', 'gates')
    return dict(zip(names, parts))


def mla_keys(ckv, krope, kv_norm_g, w_ukv, cos=None, sin=None):
    b, n, _ = ckv.shape
    kv = (rmsnorm(ckv, kv_norm_g) @ w_ukv).reshape(b, n, MLA_HEADS, QK_NOPE + V_HEAD)
    k_nope, v = jnp.split(kv, [QK_NOPE], axis=-1)
    k_pe = krope if cos is None else apply_rope(krope, cos, sin)
    return k_nope, k_pe, v


def mla_queries(cq, q_norm_g, w_uq, cos=None, sin=None):
    b, n, _ = cq.shape
    q = (rmsnorm(cq, q_norm_g) @ w_uq).reshape(b, n, MLA_HEADS, QK_NOPE + QK_ROPE)
    q_nope, q_pe = jnp.split(q, [QK_NOPE], axis=-1)
    if cos is not None:
        q_pe = apply_rope(q_pe, cos[:, None, :], sin[:, None, :])
    return q_nope, q_pe


def block_attention(q_nope, q_pe, k_nope, k_pe, v):
    b, nq, h, _ = q_nope.shape
    nb = nq // ATTN_BLOCK
    scale = (QK_NOPE + QK_ROPE) ** -0.5

    def one_block(args):
        qn, qp = args
        s = jnp.einsum('bqhd,bkhd->bhqk', qn, k_nope) + jnp.einsum('bqhr,bkr->bhqk', qp, k_pe)
        p = jax.nn.softmax(s.astype(jnp.float32) * scale, axis=-1).astype(v.dtype)
        return jnp.einsum('bhqk,bkhd->bqhd', p, v)

    qn_b = q_nope.reshape(b, nb, ATTN_BLOCK, h, QK_NOPE).swapaxes(0, 1)
    qp_b = q_pe.reshape(b, nb, ATTN_BLOCK, h, QK_ROPE).swapaxes(0, 1)
    o = lax.map(one_block, (qn_b, qp_b))
    return o.swapaxes(0, 1).reshape(b, nq, h * V_HEAD)


def fourier_mix(u):
    b, n, w = u.shape
    ug = u.astype(jnp.float32).reshape(b, n, FOURIER_GROUPS, w // FOURIER_GROUPS)
    y = jnp.fft.fft2(ug, axes=(1, 3), norm='ortho').real
    return y.reshape(b, n, w).astype(u.dtype)


def short_conv(x, w, bias):
    xp = jnp.pad(x, ((0, 0), (1, 1), (0, 0)))
    return xp[:, :-2] * w[0] + xp[:, 1:-1] * w[1] + xp[:, 2:] * w[2] + bias


def spatial_gate(u, v, ln_g, ln_b, w_s, b_s):
    b, n, w = v.shape
    vc = layernorm(v, ln_g, ln_b).reshape(b, n // CHUNK, CHUNK, SGU_GROUPS, w // SGU_GROUPS)
    mixed = jnp.einsum('gpq,bcqgd->bcpgd', w_s, vc) + b_s.T[None, None, :, :, None]
    return u * mixed.reshape(b, n, w)


def mix_branches(p, attn, conv_w, conv_b, sgu_ln_g, sgu_ln_b, sgu_w, sgu_b, w_branch, w_out):
    silu = jax.nn.silu
    ya = attn * silu(p['zA'])
    yb = fourier_mix(p['uB']) * silu(p['zB'])
    yc = p['bC'] * short_conv(p['cC'] * p['xC'], conv_w, conv_b) * silu(p['zC'])
    yd = spatial_gate(p['uD'], p['vD'], sgu_ln_g, sgu_ln_b, sgu_w, sgu_b) * silu(p['zD'])
    ys = jnp.stack([ya, yb, yc, yd], axis=2)
    b, n = ys.shape[0], ys.shape[1]
    br = jnp.einsum('bngw,gwd->bngd', ys, w_branch)
    gates = jax.nn.sigmoid(p['gates'].reshape(b, n, N_BRANCH, D_MODEL))
    merged = jnp.sum(gates * br, axis=2)
    return merged @ w_out


def setup_inputs(seed: int = 0) -> dict:
    key = jax.random.key(seed)
    ks = jax.random.split(key, 24)
    f32 = jnp.float32
    nrm = lambda k, s: jax.random.normal(k, s, dtype=f32)
    L = DEPTH
    return {
        'x': nrm(ks[0], (BATCH, SEQ, D_MODEL)),
        'c': nrm(ks[1], (BATCH, D_MODEL)),
        'ctx': nrm(ks[2], (BATCH, CTX_LEN, D_MODEL)),
        'c_ctx': nrm(ks[3], (D_MODEL,)),
        'w_mod': nrm(ks[4], (L, D_MODEL, 3 * D_MODEL)) * (0.5 * D_MODEL ** -0.5),
        'b_mod': nrm(ks[5], (L, 3 * D_MODEL)) * 0.02,
        'pre_g': 1.0 + 0.02 * nrm(ks[6], (L, D_MODEL)),
        'post_g': 1.0 + 0.02 * nrm(ks[7], (L, D_MODEL)),
        'w_in': nrm(ks[8], (L, D_MODEL, IN_WIDTH)) * D_MODEL ** -0.5,
        'q_norm_g': 1.0 + 0.02 * nrm(ks[9], (L, Q_LORA)),
        'kv_norm_g': 1.0 + 0.02 * nrm(ks[10], (L, KV_LORA)),
        'w_uq': nrm(ks[11], (L, Q_LORA, MLA_HEADS * (QK_NOPE + QK_ROPE))) * Q_LORA ** -0.5,
        'w_ukv': nrm(ks[12], (L, KV_LORA, MLA_HEADS * (QK_NOPE + V_HEAD))) * KV_LORA ** -0.5,
        'conv_w': nrm(ks[13], (L, CONV_W, BRANCH_W)) * CONV_W ** -0.5,
        'conv_b': nrm(ks[14], (L, BRANCH_W)) * 0.02,
        'sgu_ln_g': 1.0 + 0.02 * nrm(ks[15], (L, BRANCH_W)),
        'sgu_ln_b': nrm(ks[16], (L, BRANCH_W)) * 0.02,
        'sgu_w': nrm(ks[17], (L, SGU_GROUPS, CHUNK, CHUNK)) * CHUNK ** -0.5,
        'sgu_b': 1.0 + 0.02 * nrm(ks[18], (L, SGU_GROUPS, CHUNK)),
        'w_branch': nrm(ks[19], (L, N_BRANCH, BRANCH_W, D_MODEL)) * BRANCH_W ** -0.5,
        'w_out': nrm(ks[20], (L, D_MODEL, D_MODEL)) * D_MODEL ** -0.5,
    }


def reference(x, c, ctx, c_ctx, w_mod, b_mod, pre_g, post_g, w_in, q_norm_g, kv_norm_g, w_uq, w_ukv,
              conv_w, conv_b, sgu_ln_g, sgu_ln_b, sgu_w, sgu_b, w_branch, w_out):
    n = x.shape[1]
    cos, sin = axial_rope(n, x.dtype)
    for l in range(DEPTH):
        last = l == DEPTH - 1
        mod_x = jax.nn.silu(c) @ w_mod[l] + b_mod[l]
        mod_c = (jax.nn.silu(c_ctx) @ w_mod[l] + b_mod[l])[None, :]
        shift_x, scale_x, gate_x = jnp.split(mod_x, 3, axis=-1)
        shift_c, scale_c, gate_c = jnp.split(mod_c, 3, axis=-1)
        hx = modulate(rmsnorm(x, pre_g[l]), shift_x, scale_x)
        hc = modulate(rmsnorm(ctx, pre_g[l]), shift_c, scale_c)

        px = split_proj(hx @ w_in[l])
        if last:
            ckv_c, krope_c = jnp.split(hc @ w_in[l][:, :KV_COLS], [KV_LORA], axis=-1)
        else:
            pc = split_proj(hc @ w_in[l])
            ckv_c, krope_c = pc['ckv'], pc['krope']

        kn_c, kp_c, v_c = mla_keys(ckv_c, krope_c, kv_norm_g[l], w_ukv[l])
        kn_x, kp_x, v_x = mla_keys(px['ckv'], px['krope'], kv_norm_g[l], w_ukv[l], cos, sin)
        qn_x, qp_x = mla_queries(px['cq'], q_norm_g[l], w_uq[l], cos, sin)
        attn_x = block_attention(qn_x, qp_x,
                                 jnp.concatenate([kn_c, kn_x], axis=1),
                                 jnp.concatenate([kp_c, kp_x], axis=1),
                                 jnp.concatenate([v_c, v_x], axis=1))
        out_x = mix_branches(px, attn_x, conv_w[l], conv_b[l], sgu_ln_g[l], sgu_ln_b[l],
                             sgu_w[l], sgu_b[l], w_branch[l], w_out[l])

        if not last:
            qn_c, qp_c = mla_queries(pc['cq'], q_norm_g[l], w_uq[l])
            attn_c = block_attention(qn_c, qp_c, kn_c, kp_c, v_c)
            out_c = mix_branches(pc, attn_c, conv_w[l], conv_b[l], sgu_ln_g[l], sgu_ln_b[l],
                                 sgu_w[l], sgu_b[l], w_branch[l], w_out[l])
            ctx = ctx + gate_c[:, None, :] * rmsnorm(out_c, post_g[l])

        x = x + gate_x[:, None, :] * rmsnorm(out_x, post_g[l])
    return x
```

```python
import os
import collections
import numpy as np
import concourse.bass as bass
import concourse.mybir as mybir
from concourse.bass_utils import run_bass_kernel_spmd

F32 = mybir.dt.float32
BF16 = mybir.dt.bfloat16
AF = mybir.ActivationFunctionType
ALU = mybir.AluOpType

D = 2048
SEQ = 4096
CTX = 256
NT = SEQ + CTX
HALF = SEQ // 2
L = 2
INW = 14144
EPS = 1e-6
SCALE = 192.0 ** -0.5
COL = {n: 832 + 512 * i for i, n in enumerate(
    ['zA', 'uB', 'zB', 'xC', 'bC', 'cC', 'zC', 'uD', 'vD', 'zD'])}
GATE0 = 5952

V_C = 0
V_L = 32
VL_BMOD, VL_PREG, VL_POSTG, VL_QG, VL_KVG, VL_CW, VL_CB = 0, 48, 64, 80, 84, 86, 98
VL_N = 102
V_MASK = V_L + L * VL_N
NV = V_MASK + 4
NB = L * 1536


class Prog:
    ENGS = ('pe', 'act', 'dve', 'pool', 'sp')

    def __init__(self, nc):
        self.nc = nc
        self.q = {e: [] for e in self.ENGS}
        self.res = {}
        self.dcount = {}
        self.lastc = {e: -1 for e in ('pe', 'act', 'dve', 'pool')}

    def _deps(self, eng, reads, writes, is_dma):
        ev_e = {}
        ev_d = {}

        def addev(ev, kind):
            if ev is None:
                return
            if ev[0] == 'e':
                if ev[1] == eng and not is_dma:
                    if eng == 'pe' or kind != 'raw':
                        return
                if ev_e.get(ev[1], -1) < ev[2]:
                    ev_e[ev[1]] = ev[2]
            else:
                n_ = self.dcount[ev[1]] if ev[1].startswith('cv_') else ev[2]
                if ev_d.get(ev[1], 0) < n_:
                    ev_d[ev[1]] = n_
        for r in reads:
            st = self.res.get(r)
            if st:
                addev(st['w'], 'raw')
        for w in writes:
            st = self.res.get(w)
            if st:
                addev(st['w'], 'waw')
                for e2, i2 in st['re'].items():
                    addev(('e', e2, i2), 'war')
                for k2, n2 in st['rd'].items():
                    addev(('d', k2, n2), 'war')
        return ev_e, ev_d

    def _update(self, ev, reads, writes):
        for w in writes:
            self.res[w] = {'w': ev, 're': {}, 'rd': {}}
        for r in reads:
            st = self.res.setdefault(r, {'w': None, 're': {}, 'rd': {}})
            if ev[0] == 'e':
                if st['re'].get(ev[1], -1) < ev[2]:
                    st['re'][ev[1]] = ev[2]
            else:
                if st['rd'].get(ev[1], 0) < ev[2]:
                    st['rd'][ev[1]] = ev[2]

    def add(self, eng, fn, reads=(), writes=()):
        ev_e, ev_d = self._deps(eng, reads, writes, False)
        idx = len(self.q[eng])
        self.q[eng].append({'fn': fn, 'we': ev_e, 'wd': ev_d, 'dma': None})
        self.lastc[eng] = idx
        self._update(('e', eng, idx), reads, writes)

    def dma(self, eng, pairs, reads, writes, semkey, throttle=None, **kw):
        ev_e, ev_d = self._deps(eng, reads, writes, True)
        n = self.dcount.get(semkey, 0)
        if throttle is not None and n - throttle > 0:
            ev_d[semkey] = max(ev_d.get(semkey, 0), n - throttle)
        for i, (o, a) in enumerate(pairs):
            n += 1
            self.q[eng].append({
                'fn': (lambda e, o=o, a=a: e.dma_start(out=o, in_=a, **kw)),
                'we': ev_e if i == 0 else {}, 'wd': ev_d if i == 0 else {}, 'dma': semkey})
        self.dcount[semkey] = n
        self._update(('d', semkey, n), reads, writes)

    def barrier(self, final=False):
        last = dict(self.lastc)
        for e in self.ENGS:
            we = {e2: i for e2, i in last.items() if i >= 0 and e2 != e}
            wd = dict(self.dcount) if final else {k: n for k, n in self.dcount.items() if not k.startswith('cv_')}
            self.q[e].append({'fn': None, 'we': we, 'wd': wd, 'dma': None})
        self.res = {k: v for k, v in self.res.items() if k.startswith('cv_') and not final}

    def emit(self):
        nc = self.nc
        needed = {e: set() for e in self.ENGS}
        for e in self.ENGS:
            for op in self.q[e]:
                for e2, i2 in op['we'].items():
                    needed[e2].add(i2)
        esem = {e: nc.alloc_semaphore('s_' + e) for e in ('pe', 'act', 'dve', 'pool')}
        dsem = {k: nc.alloc_semaphore('d_%d' % i) for i, k in enumerate(sorted(self.dcount))}
        val = {}
        for e in ('pe', 'act', 'dve', 'pool'):
            c = 0
            for i, op in enumerate(self.q[e]):
                if i in needed[e]:
                    c += 1
                    val[(e, i)] = c
        q, dcount = self.q, self.dcount

        def run(ename, eng):
            waited = {}
            for i, op in enumerate(q[ename]):
                for e2, i2 in op['we'].items():
                    v = val[(e2, i2)]
                    if waited.get(('e', e2), 0) < v:
                        eng.wait_ge(esem[e2], v)
                        waited[('e', e2)] = v
                for k2, n2 in op['wd'].items():
                    v = 16 * n2
                    if waited.get(('d', k2), 0) < v:
                        eng.wait_ge(dsem[k2], v)
                        waited[('d', k2)] = v
                if op['fn'] is None:
                    continue
                ins = op['fn'](eng)
                if op['dma'] is not None:
                    ins.then_inc(dsem[op['dma']], 16)
                elif i in needed[ename]:
                    ins.then_inc(esem[ename], 1)
            if ename == 'sp':
                for k2, n2 in dcount.items():
                    if waited.get(('d', k2), 0) < 16 * n2:
                        eng.wait_ge(dsem[k2], 16 * n2)

        with nc.Block() as block:
            @block.tensor
            def _(e):
                run('pe', e)

            @block.scalar
            def _(e):
                run('act', e)

            @block.vector
            def _(e):
                run('dve', e)

            @block.gpsimd
            def _(e):
                run('pool', e)

            @block.sync
            def _(e):
                run('sp', e)


class Builder:
    def __init__(self, layers=(0, 1), debug=False, phases='SAQFB', blocks_limit=None):
        self.layers = layers
        self.debug = debug
        self.phases = phases
        self.blocks_limit = blocks_limit
        nc = self.nc = bass.Bass("TRN2", target_bir_lowering=False)
        self.P = Prog(nc)

        def inp(name, shape):
            return nc.dram_tensor(name, list(shape), F32, kind="ExternalInput").ap()
        self.x = inp("x", [SEQ, D])
        self.ctx = inp("ctx", [CTX, D])
        self.vecs_d = inp("vecs", [128, NV])
        self.bct_d = inp("bct", [128, NB])
        self.ident_d = inp("ident", [128, 128])
        self.rope_d = inp("ropeT", [64, 2, NT])
        self.c128_d = inp("c128s", [128, 256])
        self.tcf_d = inp("tcf", [SEQ, SEQ])
        self.tsf_d = inp("tsf", [SEQ, SEQ])
        self.tcx_d = inp("tcx", [CTX, 2, CTX])
        self.w = {}
        for l in layers:
            self.w[l] = dict(
                mod=inp("w_mod%d" % l, [D, 3 * D]), win=inp("w_in%d" % l, [D, INW]),
                uq=inp("w_uq%d" % l, [512, 768]), ukv=inp("w_ukv%d" % l, [256, 1024]),
                sgu=inp("sgu_w%d" % l, [4, 128, 128]), br=inp("w_branch%d" % l, [4, 512, D]),
                out=inp("w_out%d" % l, [D, D]))
        self.y = nc.dram_tensor("y", [HALF, D], F32, kind="ExternalOutput").ap()
        kind = "ExternalOutput" if debug else "Internal"

        def scr(name, shape, dt=BF16):
            return nc.dram_tensor(name, list(shape), dt, kind=kind).ap()
        self.KnT = scr("KnT", [512, NT])
        self.KpT = scr("KpT", [64, NT])
        self.Vd = scr("Vd", [NT, 512])
        self.FA = scr("FA", [NT, 512])
        self.FB = scr("FB", [NT, 512])
        self.QT = scr("QT", [768, NT])
        self.GT = scr("GT", [512, NT])
        self.YB = scr("YB", [512, NT])
        self.AT = scr("AT", [512, NT])
        self.X1 = scr("X1", [SEQ, D], F32)
        self.C1 = scr("C1", [CTX, D], F32)
        self.TC = nc.dram_tensor("TC", [SEQ, SEQ], BF16, kind="Internal").ap()
        self.TS = nc.dram_tensor("TS", [SEQ, SEQ], BF16, kind="Internal").ap()
        self.CW = {}
        for l in layers:
            self.CW[l] = dict(
                win=nc.dram_tensor("cw_in%d" % l, [28, 128, 16, 512], BF16, kind="Internal").ap(),
                br=nc.dram_tensor("cw_br%d" % l, [16, 128, 4, 512], BF16, kind="Internal").ap(),
                out=nc.dram_tensor("cw_out%d" % l, [4, 128, 16, 512], BF16, kind="Internal").ap(),
                mod=nc.dram_tensor("cw_mod%d" % l, [12, 128, 16, 512], BF16, kind="Internal").ap())
        self.conv = {}
        self.bg_tasks = collections.deque()
        self.psn = 0
        self.wn = 0
        self.ncnt = 0
        self.off = (nc.sbuf_base + 63) // 64 * 64
        self.sb_top = nc.sbuf_top

    def sb(self, name, shape, dt, at=None):
        nbytes = int(np.prod(shape[1:])) * (4 if dt == F32 else 2)
        if at is None:
            off = (self.off + 63) // 64 * 64
            self.off = off + nbytes
            assert self.off <= self.sb_top, ("SBUF overflow", name, self.off)
        else:
            off = at
        self.ncnt += 1
        t = self.nc.alloc_sbuf_tensor_at("%s_%d" % (name, self.ncnt), list(shape), dt, offset=off)
        self.last_off = off
        return t

    def ps_next(self, pool=(0, 1, 2, 3, 4, 5)):
        b = pool[self.psn % len(pool)]
        self.psn += 1
        return b

    def act(self, out, in_, func, reads, writes, bias=None, scale=None, accum_out=None):
        kw = {}
        if bias is not None:
            kw['bias'] = bias
        if scale is not None:
            kw['scale'] = scale
        if accum_out is not None:
            kw['accum_out'] = accum_out
        self.P.add('act', lambda e: e.activation(out=out, in_=in_, func=func, **kw), reads, writes)

    def tt(self, out, in0, in1, op, reads, writes, eng='dve'):
        self.P.add(eng, lambda e: e.tensor_tensor(out=out, in0=in0, in1=in1, op=op), reads, writes)

    def ts(self, out, in0, s1, s2, op0, op1, reads, writes, eng='dve'):
        if s2 is None:
            self.P.add(eng, lambda e: e.tensor_scalar(out=out, in0=in0, scalar1=s1, scalar2=None,
                                                      op0=op0), reads, writes)
        else:
            self.P.add(eng, lambda e: e.tensor_scalar(out=out, in0=in0, scalar1=s1, scalar2=s2,
                                                      op0=op0, op1=op1), reads, writes)

    def stt(self, out, in0, s, in1, op0, op1, reads, writes, eng='dve'):
        self.P.add(eng, lambda e: e.scalar_tensor_tensor(out=out, in0=in0, scalar=s, in1=in1,
                                                         op0=op0, op1=op1), reads, writes)

    def mm(self, out, lhsT, rhs, start, stop, reads, writes):
        self.P.add('pe', lambda e: e.matmul(out, lhsT, rhs, start=start, stop=stop), reads, writes)

    def rsqrt_mean(self, out, in_, n, reads, writes):
        self.ts(out, in_, 1.0 / n, EPS, ALU.mult, ALU.add, reads, writes)
        self.act(out, out, AF.Sqrt, writes, writes)
        self.P.add('dve', lambda e: e.reciprocal(out=out, in_=out), writes, writes)

    @staticmethod
    def in_entry(e):
        return (0, 320) if e == 0 else (320 + 512 * (e - 1), 512)

    def wload(self, l, kind, idx, pieces=None):
        b = self.wn % len(self.W)
        self.wn += 1
        if self.wn % 3 == 0 and self.bg_tasks:
            self.bg_tasks.popleft()()
        W = self.W[b]
        if kind == 'win':
            c0, width = self.in_entry(idx)
        else:
            c0, width = idx * 512, 512
        if pieces is None:
            pieces = [(0, width, 0)]
        pairs = []
        if self.conv.get((l, kind)):
            src = self.CW[l][kind][idx]
            for (o, n, d0) in pieces:
                pairs.append((W[:, :, d0:d0 + n], src[:, :, o:o + n]))
            reads = ['cv_%s%d' % (kind, l)]
        else:
            wd = self.w[l][kind]
            for (o, n, d0) in pieces:
                pairs.append((W[:, :, d0:d0 + n],
                              wd[:, c0 + o:c0 + o + n].rearrange("(k p) c -> p k c", p=128)))
            reads = []
        self.P.dma('pool', pairs, reads=reads, writes=['W%d' % b], semkey='W%d' % b)
        return W, 'W%d' % b

    def convert_weights(self, l, kinds, in_entries=None, defer=False):
        if defer:
            for kind in kinds:
                self.bg_tasks.extend(self._convert_tasks(l, kind, in_entries))
            return
        for kind in kinds:
            for t in self._convert_tasks(l, kind, in_entries):
                t()

    def _convert_tasks(self, l, kind, in_entries=None):
        P = self.P
        w = self.w[l]
        tasks = []
        key = 'cv_%s%d' % (kind, l)
        dst = self.CW[l][kind]

        def dma_task(d, s_, rk):
            return lambda: P.dma('pool', [(d, s_)], [], [rk], key, throttle=2)
        if kind == 'win':
            ents = in_entries if in_entries is not None else list(range(28))
            for e in ents:
                c0, width = self.in_entry(e)
                tasks.append(dma_task(dst[e][:, :, 0:width], w['win'][:, c0:c0 + width].rearrange("(k p) c -> p k c", p=128), '%s_%d' % (key, e)))
        elif kind == 'br':
            for g in range(4):
                for jd in range(4):
                    tasks.append(dma_task(dst[g * 4 + jd], w['br'][g, :, jd * 512:(jd + 1) * 512].rearrange("(k p) c -> p k c", p=128),
                                          '%s_%d' % (key, g * 4 + jd)))
        else:
            nch = 12 if kind == 'mod' else 4
            for ci in range(nch):
                tasks.append(dma_task(dst[ci], w[kind][:, ci * 512:(ci + 1) * 512].rearrange("(k p) c -> p k c", p=128), '%s_%d' % (key, ci)))

        def fin():
            P.res[key] = {'w': ('d', key, P.dcount[key]), 're': {}, 'rd': {}}
            self.conv[(l, kind)] = True
        tasks.append(fin)
        return tasks

    def bg_flush(self):
        while self.bg_tasks:
            self.bg_tasks.popleft()()

    def _old_convert_weights(self, l, kinds, in_entries=None):
        P = self.P
        w = self.w[l]
        for kind in kinds:
            key = 'cv_%s%d' % (kind, l)
            dst = self.CW[l][kind]
            if kind == 'win':
                ents = in_entries if in_entries is not None else list(range(28))
                for e in ents:
                    c0, width = self.in_entry(e)
                    P.dma('pool', [(dst[e][:, :, 0:width], w['win'][:, c0:c0 + width].rearrange("(k p) c -> p k c", p=128))],
                          [], ['%s_%d' % (key, e)], key)
            elif kind == 'br':
                for g in range(4):
                    for jd in range(4):
                        P.dma('pool', [(dst[g * 4 + jd], w['br'][g, :, jd * 512:(jd + 1) * 512].rearrange("(k p) c -> p k c", p=128))],
                              [], ['%s_%d' % (key, g * 4 + jd)], key)
            else:
                nch = 12 if kind == 'mod' else 4
                for ci in range(nch):
                    P.dma('pool', [(dst[ci], w[kind][:, ci * 512:(ci + 1) * 512].rearrange("(k p) c -> p k c", p=128))],
                          [], ['%s_%d' % (key, ci)], key)
            P.res[key] = {'w': ('d', key, P.dcount[key]), 're': {}, 'rd': {}}
            self.conv[(l, kind)] = True

    def convert_tables(self):
        P = self.P
        for j in range(32):
            for (dst, src) in ((self.TC, self.tcf_d), (self.TS, self.tsf_d)):
                P.dma('pool', [(dst[j * 128:(j + 1) * 128, :].rearrange("r (a b) -> r a b", b=2048),
                                src[j * 128:(j + 1) * 128, :].rearrange("r (a b) -> r a b", b=2048))],
                      [], ['cv_T_%d' % j], 'cv_T', throttle=8)
        P.res['cv_T'] = {'w': ('d', 'cv_T', P.dcount['cv_T']), 're': {}, 'rd': {}}

    def build(self):
        nc, P = self.nc, self.P
        self.vecs = self.sb("vecs", [128, NV], F32)
        self.ident_f = self.sb("ident_f", [128, 128], F32)
        self.ident_b = self.sb("ident_b", [128, 128], BF16)
        self.ones_f = self.sb("ones_f", [128, 128], F32)
        self.ones_b = self.sb("ones_b", [128, 128], BF16)
        self.c128 = self.sb("c128", [128, 256], BF16)
        self.tcx = self.sb("tcx", [128, 2, 2, CTX], BF16)
        self.modT = self.sb("modT", [128, 48, 2], F32)
        self.G = self.sb("G", [128, 16, 2], F32)
        self.GP = self.sb("GP", [128, 16, 2], F32)
        self.GPb = self.sb("GPb", [128, 2, D], F32)
        self.sc = self.sb("sc", [128, 16, 2], BF16)
        self.WsT = self.sb("WsT", [128, 4, 128], BF16)
        self.lnb = self.sb("lnb", [128, 1536], F32)
        self.jk = self.sb("jk", [128, 512], BF16)
        self.ps = [nc.alloc_psum_tensor("ps%d" % i, [128, 512], F32) for i in range(7)]
        self.pst = nc.alloc_psum_tensor("pst", [128, 1024], BF16)
        self.base_off = self.off

        self.setup0()
        self.off = self.base_off
        self.W = [self.sb("W%d" % i, [128, 16, 512], BF16) for i in range(2)]
        self.base_off = self.off
        for l in self.layers:
            last = (l == L - 1)
            src_x = self.x if l == 0 else self.X1
            src_c = self.ctx if l == 0 else self.C1
            blocks = [(k * 512, 512, 0) for k in range(8)] + [(SEQ, CTX, 1)]
            if self.blocks_limit:
                blocks = blocks[:self.blocks_limit]
            bblocks = blocks[:4] if last else blocks
            for ph in self.phases:
                self.off = self.base_off
                if ph == 'S':
                    self.setup_layer(l)
                elif ph == 'A':
                    self.phaseA(l, blocks, src_x, src_c, last)
                elif ph == 'F':
                    self.phaseF(l, bblocks)
                elif ph == 'Q':
                    if l == self.layers[0]:
                        self.convert_tables()
                        self.convert_weights(l, ['win'], in_entries=[2 + i for i in (0, 2, 6, 4, 8, 7, 9)] + list(range(12, 28)))
                        self.convert_weights(l, ['br', 'out'])
                        for l2 in self.layers[1:]:
                            self.convert_weights(l2, ['mod', 'win', 'br', 'out'], defer=True)
                    self.phaseQ(l, bblocks)
                elif ph == 'B':
                    self.phaseB(l, bblocks, src_x, src_c, last)
                    self.bg_flush()
                P.barrier()
        P.barrier(final=True)
        P.emit()
        return nc

    def setup0(self):
        P, nc = self.P, self.nc
        P.dma('sp', [(self.vecs[:], self.vecs_d)], [], ['vecs'], 'vecs')
        P.dma('sp', [(self.ident_f[:], self.ident_d)], [], ['ident_f'], 'ident_f')
        P.add('dve', lambda e: e.tensor_copy(out=self.ident_b[:], in_=self.ident_f[:]),
              ['ident_f'], ['ident_b'])
        P.add('dve', lambda e: e.memset(self.ones_f[:], 1.0), [], ['ones_f'])
        P.add('dve', lambda e: e.memset(self.ones_b[:], 1.0), [], ['ones_b'])
        P.dma('pool', [(self.c128[:], self.c128_d)], [], ['c128'], 'c128')
        P.dma('pool', [(self.tcx[:, j, :, :], self.tcx_d[j * 128:(j + 1) * 128, :, :]) for j in range(2)],
              [], ['tcx'], 'tcx')
        if self.debug:
            tch = self.sb("tch", [1, 64], F32)
            aps = [self.x, self.ctx, self.bct_d, self.c128_d, self.tcf_d, self.tsf_d]
            for l in self.layers:
                aps += [self.w[l][k] for k in ('mod', 'win', 'uq', 'ukv', 'br', 'out', 'sgu')]
            prs = []
            for i, a in enumerate(aps):
                src = a[0:1, 0:1] if len(a.shape) == 2 else a[0, 0:1, 0:1]
                prs.append((tch[0:1, i:i + 1], src))
            prs.append((tch[0:1, 40:41], self.rope_d[0:1, 0, 0:1]))
            prs.append((tch[0:1, 41:42], self.tcx_d[0:1, 0, 0:1]))
            P.dma('sp', prs, [], ['tch'], 'tch')
        P.barrier()

    def gen_tables(self):
        P = self.P
        HW = SEQ // 2
        cp = self.sb("g_cp", [128, SEQ], F32)
        sp_ = self.sb("g_sp", [128, SEQ], F32)
        cjb = [self.sb("g_cj%d" % i, [128, HW], F32) for i in range(2)]
        sjb = [self.sb("g_sj%d" % i, [128, HW], F32) for i in range(2)]
        t1 = self.sb("g_t1", [128, HW], F32)
        t2 = self.sb("g_t2", [128, HW], F32)
        oc = [self.sb("g_oc%d" % i, [128, HW], BF16) for i in range(2)]
        os_ = [self.sb("g_os%d" % i, [128, HW], BF16) for i in range(2)]
        P.dma('sp', [(cp[:], self.cp_d)], [], ['g_cp'], 'g_cp')
        P.dma('sp', [(sp_[:], self.spp_d)], [], ['g_sp'], 'g_sp')
        it = 0
        for j in range(32):
            for hf in range(2):
                b = it % 2
                it += 1
                cs = slice(hf * HW, (hf + 1) * HW)
                P.dma('sp', [(cjb[b][:], self.cj_d[j, cs].partition_broadcast(128))], [], ['g_cj%d' % b], 'g_cj%d' % b)
                P.dma('sp', [(sjb[b][:], self.sj_d[j, cs].partition_broadcast(128))], [], ['g_sj%d' % b], 'g_sj%d' % b)
                self.tt(t1[:], cp[:, cs], cjb[b][:], ALU.mult, ['g_cp', 'g_cj%d' % b], ['g_t1'])
                self.tt(t2[:], sp_[:, cs], sjb[b][:], ALU.mult, ['g_sp', 'g_sj%d' % b], ['g_t2'], eng='pool')
                self.tt(oc[b][:], t1[:], t2[:], ALU.subtract, ['g_t1', 'g_t2'], ['g_oc%d' % b])
                P.dma('sp', [(self.TC[j * 128:(j + 1) * 128, cs], oc[b][:])], ['g_oc%d' % b], ['TC'], 'g_oc%d' % b)
                self.tt(t1[:], sp_[:, cs], cjb[b][:], ALU.mult, ['g_sp', 'g_cj%d' % b], ['g_t1'])
                self.tt(t2[:], cp[:, cs], sjb[b][:], ALU.mult, ['g_cp', 'g_sj%d' % b], ['g_t2'], eng='pool')
                self.stt(os_[b][:], t1[:], -1.0, t2[:], ALU.mult, ALU.subtract, ['g_t1', 'g_t2'], ['g_os%d' % b])
                P.dma('sp', [(self.TS[j * 128:(j + 1) * 128, cs], os_[b][:])], ['g_os%d' % b], ['TS'], 'g_os%d' % b)

    def setup_layer(self, l):
        P = self.P
        w = self.w[l]
        vl = V_L + l * VL_N
        vecs = self.vecs
        for j in range(2):
            self.act(self.sc[:, :, j], vecs[:, V_C + 16 * j:V_C + 16 * j + 16], AF.Silu, ['vecs'], ['sc'])
        for ci in range(12):
            W, wk = self.wload(l, 'mod', ci)
            for g in range(4):
                cg = ci * 4 + g
                b = self.ps_next()
                for k in range(16):
                    self.mm(self.ps[b][:, 0:2], W[:, k, g * 128:(g + 1) * 128], self.sc[:, k, :],
                            k == 0, k == 15, [wk, 'sc'], ['ps%d' % b])
                self.ts(self.modT[:, cg, :], self.ps[b][:, 0:2], vecs[:, vl + VL_BMOD + cg:vl + VL_BMOD + cg + 1],
                        None, ALU.add, None, ['ps%d' % b, 'vecs'], ['modT'])
        for j in range(2):
            self.ts(self.G[:, :, j], self.modT[:, 16:32, j], 1.0, None, ALU.add, None, ['modT'], ['G'])
            self.tt(self.G[:, :, j], self.G[:, :, j], vecs[:, vl + VL_PREG:vl + VL_PREG + 16], ALU.mult,
                    ['G', 'vecs'], ['G'])
            self.tt(self.GP[:, :, j], self.modT[:, 32:48, j], vecs[:, vl + VL_POSTG:vl + VL_POSTG + 16],
                    ALU.mult, ['modT', 'vecs'], ['GP'])
        dg = self.sb("dg", [128, 4, 128], F32)
        for j in range(2):
            for q4 in range(4):
                b = self.ps_next()
                for i4 in range(4):
                    i = q4 * 4 + i4
                    self.ts(dg[:, i4, :], self.ident_f[:], self.GP[:, i, j:j + 1], None, ALU.mult, None,
                            ['ident_f', 'GP'], ['dg%d' % i4])
                    self.mm(self.ps[b][:, i4 * 128:(i4 + 1) * 128], self.ones_f[:], dg[:, i4, :], True, True,
                            ['ones_f', 'dg%d' % i4], ['ps%d' % b])
                self.P.add('act', lambda e, b=b, j=j, q4=q4: e.copy(out=self.GPb[:, j, q4 * 512:(q4 + 1) * 512],
                                                                   in_=self.ps[b][:, :]),
                           ['ps%d' % b], ['GPb'])
        st = self.sb("wst", [128, 4, 128], F32)
        P.dma('sp', [(st[:, g, :], w['sgu'][g, :, :]) for g in range(4)], [], ['wst'], 'wst')
        wsb = self.sb("wsb", [128, 4, 128], BF16)
        P.add('dve', lambda e: e.tensor_copy(out=wsb[:], in_=st[:]), ['wst'], ['wsb'])
        for g in range(4):
            P.add('pe', lambda e, g=g: e.transpose(self.pst[:, g * 128:(g + 1) * 128], wsb[:, g, :], self.ident_b[:]),
                  ['wsb', 'ident_b'], ['pst'])
        for g in range(4):
            P.add('dve', lambda e, g=g: e.tensor_copy(out=self.WsT[:, g, :], in_=self.pst[:, g * 128:(g + 1) * 128]),
                  ['pst'], ['WsT'])
        P.dma('sp', [(self.lnb[:], self.bct_d[:, l * 1536:(l + 1) * 1536])], [], ['lnb'], 'lnb')

    def prep_attn_weights(self, l):
        P = self.P
        w = self.w[l]
        vl = V_L + l * VL_N
        vecs = self.vecs
        self.Wkv = self.sb("Wkv", [128, 2, 1024], BF16)
        self.Wq = self.sb("Wq", [128, 4, 4, 256], BF16)
        mark = self.off
        st = self.sb("wst2", [128, 4, 1024], F32)
        P.dma('sp', [(st[:, c, :], w['ukv'][c * 128:(c + 1) * 128, :]) for c in range(2)], [], ['wst2'], 'wst2')
        for c in range(2):
            gsc = vecs[:, vl + VL_KVG + c:vl + VL_KVG + c + 1]
            for t in range(2):
                self.ts(self.Wkv[:, c, t * 512:(t + 1) * 512].rearrange("p (h d) -> p h d", h=4),
                        st[:, c, :].rearrange("p (h t d) -> p h t d", h=4, t=2)[:, :, t, :],
                        gsc, None, ALU.mult, None, ['wst2', 'vecs'], ['Wkv'])
        P.dma('sp', [(st[:, c, 0:768], w['uq'][c * 128:(c + 1) * 128, :]) for c in range(4)], [], ['wst2'], 'wst2')
        for c in range(4):
            gsc = vecs[:, vl + VL_QG + c:vl + VL_QG + c + 1]
            s3 = st[:, c, 0:768].rearrange("p (h d) -> p h d", h=4)
            self.ts(self.Wq[:, c, :, 0:192], s3, gsc, None, ALU.mult, None, ['wst2', 'vecs'], ['Wq'])
            self.ts(self.Wq[:, c, :, 192:224], s3[:, :, 160:192], gsc, -1.0, ALU.mult, ALU.mult,
                    ['wst2', 'vecs'], ['Wq'])
            self.ts(self.Wq[:, c, :, 224:256], s3[:, :, 128:160], gsc, None, ALU.mult, None,
                    ['wst2', 'vecs'], ['Wq'])
        P.barrier()
        self.off = mark

    def alloc_hx(self, hxTs=None):
        self.xs = [self.sb("xs%d" % i, [128, D], F32) for i in range(2)]
        self.xn = [self.sb("xn%d" % i, [128, D], BF16) for i in range(2)]
        self.ss = self.sb("ss", [128, 8], F32)
        self.hxTs = hxTs if hxTs is not None else [self.sb("hxT%d" % i, [128, 16, 512], BF16) for i in range(2)]
        self.hcur = 0
        self.hxT = self.hxTs[0]
        self.hk = 'hxT0'
        self.ecnt = 0

    def hx_stage1(self, src, r0, s):
        P = self.P
        b = s % 2
        xs, xn = self.xs[b], self.xn[b]
        rr = r0 + s * 128
        P.dma('sp', [(xs[:], src[rr:rr + 128, :])], ['xsrc'], ['xs%d' % b], 'xs%d' % b)
        ssv = self.ss[:, b:b + 1]
        self.act(xn[:], xs[:], AF.Square, ['xs%d' % b], ['xn%d' % b, 'ss%d' % b], accum_out=ssv)
        self.rsqrt_mean(self.ss[:, 4 + b:5 + b], ssv, D, ['ss%d' % b], ['rr%d' % b])
        self.act(xn[:], xs[:], AF.Copy, ['xs%d' % b, 'rr%d' % b], ['xn%d' % b], scale=self.ss[:, 4 + b:5 + b])

    def hx_stage2(self, hb, s, j):
        P = self.P
        b = s % 2
        xn = self.xn[b]
        hxT = self.hxTs[hb]
        hk = 'hxT%d' % hb
        for i0 in (0, 8):
            self.tbank = 1 - getattr(self, 'tbank', 0)
            if self.tbank:
                tb_, tkey = self.pst, 'pst'
            else:
                tb_, tkey = self.ps[6][:, :].bitcast(BF16), 'ps6'
            for i in range(i0, i0 + 8):
                pt = tb_[:, (i - i0) * 128:(i - i0 + 1) * 128]
                P.add('pe', lambda e, pt=pt, xn=xn, i=i: e.transpose(pt, xn[:, i * 128:(i + 1) * 128], self.ident_b[:]),
                      ['xn%d' % b, 'ident_b'], [tkey])
            for i in range(i0, i0 + 8):
                pt = tb_[:, (i - i0) * 128:(i - i0 + 1) * 128]
                o = hxT[:, i, s * 128:(s + 1) * 128]
                gi = self.G[:, i, j:j + 1]
                si = self.modT[:, i, j:j + 1]
                self.ecnt += 1
                if self.ecnt % 2 == 0:
                    self.act(o, pt, AF.Identity, [tkey, 'G', 'modT'], [hk], bias=si, scale=gi)
                else:
                    self.ts(o, pt, gi, si, ALU.mult, ALU.add, [tkey, 'G', 'modT'], [hk])

    def hx_full(self, src, r0, n, j, hb):
        for s in range(n // 128):
            self.hx_stage1(src, r0, s)
            self.hx_stage2(hb, s, j)

    def hx_pipeline(self, blocks, srcs):
        state = {'p': 0}

        def start_block(bi):
            state['p'] = 0
            self.hcur = bi % 2
            self.hxT = self.hxTs[self.hcur]
            self.hk = 'hxT%d' % self.hcur
            if bi == 0:
                tok0, n, j = blocks[0]
                self.hx_full(srcs[j], 0 if j else tok0, n, j, 0)

        def step(bi):
            p = state['p']
            state['p'] += 1
            if bi + 1 >= len(blocks):
                return
            tok0, n, j = blocks[bi + 1]
            ns = n // 128
            src, r0 = srcs[j], (0 if j else tok0)
            hb = (bi + 1) % 2
            if p < ns:
                self.hx_stage1(src, r0, p)
            if 1 <= p <= ns:
                self.hx_stage2(hb, p - 1, j)

        def finish(bi):
            while state['p'] <= 4:
                step(bi)
        return start_block, step, finish

    def proj_fm(self, W, wk, c0, M, n, b):
        for k in range(16):
            self.mm(self.ps[b][0:M, 0:n], W[:, k, c0:c0 + M], self.hxT[:, k, 0:n], k == 0, k == 15,
                    [wk, self.hk], ['ps%d' % b])

    def cp_act(self, out, in_, reads, writes):
        self.P.add('act', lambda e: e.copy(out=out, in_=in_), reads, writes)

    def cp_dve(self, out, in_, reads, writes):
        self.P.add('dve', lambda e: e.tensor_copy(out=out, in_=in_), reads, writes)

    def phaseA(self, l, blocks, src_x, src_c, last):
        P = self.P
        self.prep_attn_weights(l)
        self.alloc_hx()
        rope = self.sb("rope", [64, 2, 512], F32)
        cfk = self.sb("cfk", [128, 2, 512], BF16)
        sqk = self.sb("sqk", [128, 2, 512], BF16)
        rbk = self.sb("rbk", [128, 512], F32)
        cnk = self.sb("cnk", [128, 2, 512], BF16)
        cfq = self.sb("cfq", [128, 4, 512], BF16)
        sqq = self.sb("sqq", [128, 4, 512], BF16)
        rbq = self.sb("rbq", [128, 512], F32)
        cnq = self.sb("cnq", [128, 4, 512], BF16)
        xcb = self.sb("xcb", [128, 4, 512], BF16)
        kn_st = self.sb("kn_st", [128, 4, 512], BF16)
        kp_st = self.sb("kp_st", [64, 512], BF16)
        v_st = self.sb("v_st", [128, 4, 512], BF16)
        qn_st = self.sb("qn_st", [128, 4, 512], BF16)
        qp_st = self.sb("qp_st", [64, 4, 512], BF16)
        t1 = self.sb("a_t1", [64, 512], F32)
        t2 = self.sb("a_t2", [64, 512], F32)
        ub = self.sb("ub", [128, 4, 512], BF16)
        fa_st = self.sb("fa_st", [128, 4, 512], BF16)
        fb_st = self.sb("fb_st", [128, 4, 512], BF16)
        g_st = self.sb("g_st", [128, 4, 512], BF16)

        def rope_apply(pa, pb, out, okey, n):
            self.tt(t1[:, 0:n], self.ps[pa][0:64, 0:n], rope[:, 0, 0:n], ALU.mult, ['ps%d' % pa, 'rope'], ['a_t1'])
            self.tt(t2[:, 0:n], self.ps[pb][0:64, 0:n], rope[:, 1, 0:n], ALU.mult, ['ps%d' % pb, 'rope'], ['a_t2'])
            self.tt(out, t1[:, 0:n], t2[:, 0:n], ALU.add, ['a_t1', 'a_t2'], [okey])

        def norm_fm(nch, n, dim, cf, sq, rb, cn, tag):
            b = self.ps_next()
            for c in range(nch):
                self.mm(self.ps[b][:, 0:n], self.ones_b[:], sq[:, c, 0:n], c == 0, c == nch - 1,
                        ['ones_b', 'sq' + tag], ['ps%d' % b])
            self.rsqrt_mean(rb[:, 0:n], self.ps[b][:, 0:n], dim, ['ps%d' % b], ['rb' + tag])
            for c in range(nch):
                self.tt(cn[:, c, 0:n], cf[:, c, 0:n], rb[:, 0:n], ALU.mult, ['cf' + tag, 'rb' + tag], ['cn' + tag])

        srcs = {0: src_x, 1: src_c}
        start_block, step, finish = self.hx_pipeline(blocks, srcs)
        for bi, (tok0, n, j) in enumerate(blocks):
            start_block(bi)
            ns = n // 128
            full = not (last and j)
            need_q = not (last and bi >= 4)
            need_g = not (last and bi in (5, 6))
            P.dma('sp', [(rope[:, :, 0:n], self.rope_d[:, :, tok0:tok0 + n])], [], ['rope'], 'rope')
            W, wk = self.wload(l, 'win', 0, [(0, 320, 0), (288, 32, 320), (256, 32, 352)])
            self.ts(W[:, :, 320:352], W[:, :, 320:352], -1.0, None, ALU.mult, None, [wk], [wk])
            for c in range(2):
                b = self.ps_next()
                self.proj_fm(W, wk, c * 128, 128, n, b)
                self.cp_act(cfk[:, c, 0:n], self.ps[b][:, 0:n], ['ps%d' % b], ['cfk'])
                self.act(sqk[:, c, 0:n], self.ps[b][:, 0:n], AF.Square, ['ps%d' % b], ['sqk'])
            pa = self.ps_next()
            self.proj_fm(W, wk, 256, 64, n, pa)
            pb = self.ps_next()
            self.proj_fm(W, wk, 320, 64, n, pb)
            rope_apply(pa, pb, kp_st[:, 0:n], 'kp_st', n)
            P.dma('sp', [(self.KpT[:, tok0:tok0 + n], kp_st[:, 0:n])], ['kp_st'], ['KpT'], 'kp_st')
            norm_fm(2, n, 256, cfk, sqk, rbk, cnk, 'k')
            step(bi)
            if full and need_q:
                W, wk = self.wload(l, 'win', 1)
                for c in range(4):
                    b = self.ps_next()
                    self.proj_fm(W, wk, c * 128, 128, n, b)
                    self.cp_act(cfq[:, c, 0:n], self.ps[b][:, 0:n], ['ps%d' % b], ['cfq'])
                    self.act(sqq[:, c, 0:n], self.ps[b][:, 0:n], AF.Square, ['ps%d' % b], ['sqq'])
                norm_fm(4, n, 512, cfq, sqq, rbq, cnq, 'q')
            step(bi)
            for h in range(4):
                b = self.ps_next()
                for c in range(2):
                    self.mm(self.ps[b][:, 0:n], self.Wkv[:, c, h * 128:(h + 1) * 128], cnk[:, c, 0:n],
                            c == 0, c == 1, ['Wkv', 'cnk'], ['ps%d' % b])
                self.cp_act(kn_st[:, h, 0:n], self.ps[b][:, 0:n], ['ps%d' % b], ['kn_st'])
            P.dma('sp', [(self.KnT.rearrange("(h p) t -> p h t", p=128)[:, :, tok0:tok0 + n], kn_st[:, :, 0:n])],
                  ['kn_st'], ['KnT'], 'kn_st')
            for s in range(ns):
                b = self.ps_next()
                for c in range(2):
                    self.mm(self.ps[b][:, :], cnk[:, c, s * 128:(s + 1) * 128], self.Wkv[:, c, 512:1024],
                            c == 0, c == 1, ['Wkv', 'cnk'], ['ps%d' % b])
                self.cp_dve(v_st[:, s, :], self.ps[b][:, :], ['ps%d' % b], ['v_st'])
            P.dma('sp', [(self.Vd[tok0:tok0 + n, :].rearrange("(s p) d -> p s d", p=128), v_st[:, 0:ns, :])],
                  ['v_st'], ['Vd'], 'v_st')
            if not full:
                finish(bi)
                continue
            W, wk = self.wload(l, 'win', 2 + (COL['uB'] - 832) // 512)
            for c in range(4):
                b = self.ps_next()
                self.proj_fm(W, wk, c * 128, 128, n, b)
                self.cp_act(ub[:, c, 0:n], self.ps[b][:, 0:n], ['ps%d' % b], ['ub'])
            step(bi)
            for h in (range(4) if need_q else ()):
                b = self.ps_next()
                for c in range(4):
                    self.mm(self.ps[b][:, 0:n], self.Wq[:, c, h, 0:128], cnq[:, c, 0:n], c == 0, c == 3,
                            ['Wq', 'cnq'], ['ps%d' % b])
                self.cp_act(qn_st[:, h, 0:n], self.ps[b][:, 0:n], ['ps%d' % b], ['qn_st'])
                pa = self.ps_next()
                for c in range(4):
                    self.mm(self.ps[pa][0:64, 0:n], self.Wq[:, c, h, 128:192], cnq[:, c, 0:n], c == 0, c == 3,
                            ['Wq', 'cnq'], ['ps%d' % pa])
                pb = self.ps_next()
                for c in range(4):
                    self.mm(self.ps[pb][0:64, 0:n], self.Wq[:, c, h, 192:256], cnq[:, c, 0:n], c == 0, c == 3,
                            ['Wq', 'cnq'], ['ps%d' % pb])
                rope_apply(pa, pb, qp_st[:, h, 0:n], 'qp_st', n)
            if need_q:
                P.dma('sp', [(self.QT[0:512, :].rearrange("(h p) t -> p h t", p=128)[:, :, tok0:tok0 + n], qn_st[:, :, 0:n]),
                             (self.QT[512:768, :].rearrange("(h p) t -> p h t", p=64)[:, :, tok0:tok0 + n], qp_st[:, :, 0:n])],
                      ['qn_st', 'qp_st'], ['QT'], 'q_st')
            if need_g:
                W, wk = self.wload(l, 'win', 2 + (COL['xC'] - 832) // 512)
                for c in range(4):
                    b = self.ps_next()
                    self.proj_fm(W, wk, c * 128, 128, n, b)
                    self.cp_act(xcb[:, c, 0:n], self.ps[b][:, 0:n], ['ps%d' % b], ['xcb'])
            step(bi)
            for s in range(ns):
                for (tsel, stt_, skey) in ((0, fa_st, 'fa_st'), (1, fb_st, 'fb_st')):
                    b = self.ps_next()
                    for g in range(4):
                        self.mm(self.ps[b][:, g * 128:(g + 1) * 128], ub[:, g, s * 128:(s + 1) * 128],
                                self.c128[:, tsel * 128:(tsel + 1) * 128], True, True, ['ub', 'c128'], ['ps%d' % b])
                    self.cp_dve(stt_[:, s, :], self.ps[b][:, :], ['ps%d' % b], [skey])
            P.dma('sp', [(self.FA[tok0:tok0 + n, :].rearrange("(s p) d -> p s d", p=128), fa_st[:, 0:ns, :])],
                  ['fa_st'], ['FA'], 'fa_st')
            P.dma('sp', [(self.FB[tok0:tok0 + n, :].rearrange("(s p) d -> p s d", p=128), fb_st[:, 0:ns, :])],
                  ['fb_st'], ['FB'], 'fb_st')
            if need_g:
                W, wk = self.wload(l, 'win', 2 + (COL['cC'] - 832) // 512)
                for c in range(4):
                    b = self.ps_next()
                    self.proj_fm(W, wk, c * 128, 128, n, b)
                    self.tt(g_st[:, c, 0:n], self.ps[b][:, 0:n], xcb[:, c, 0:n], ALU.mult, ['ps%d' % b, 'xcb'], ['g_st'])
                P.dma('sp', [(self.GT.rearrange("(c p) t -> p c t", p=128)[:, :, tok0:tok0 + n], g_st[:, :, 0:n])],
                      ['g_st'], ['GT'], 'g_st')
            finish(bi)

    def phaseF(self, l, blocks):
        P = self.P
        fa = self.sb("fa", [128, 34, 512], BF16)
        fb = self.sb("fb", [128, 34, 512], BF16)
        tb = [self.sb("tb%d" % i, [128, 2, 4, 512], BF16) for i in range(4)]
        yb_st = [self.sb("yb_st%d" % i, [128, 4, 512], BF16) for i in range(2)]
        P.dma('sp', [(fa[:, q * 17:(q + 1) * 17, :], self.FA[q * 17 * 128:(q + 1) * 17 * 128, :].rearrange("(s p) d -> p s d", p=128))
                     for q in range(2)], ['FA'], ['fa'], 'fa')
        P.dma('sp', [(fb[:, q * 17:(q + 1) * 17, :], self.FB[q * 17 * 128:(q + 1) * 17 * 128, :].rearrange("(s p) d -> p s d", p=128))
                     for q in range(2)], ['FB'], ['fb'], 'fb')
        tbn = 0
        acc = (0, 1, 2, 3)
        for bi, (tok0, n, j) in enumerate(blocks):
            ys = yb_st[bi % 2]
            yk = 'yb_st%d' % (bi % 2)
            if not j:
                for jg in range(8):
                    t = tbn % 4
                    tbn += 1
                    P.dma('sp', [(tb[t][:, 0, :, :], self.TC[jg * 512:(jg + 1) * 512, tok0:tok0 + 512].rearrange("(q p) n -> p q n", p=128)),
                                 (tb[t][:, 1, :, :], self.TS[jg * 512:(jg + 1) * 512, tok0:tok0 + 512].rearrange("(q p) n -> p q n", p=128))],
                          ['cv_T'], ['tb%d' % t], 'tb%d' % t)
                    for q in range(4):
                        jj = jg * 4 + q
                        for c in range(4):
                            self.mm(self.ps[acc[c]][:, :], fa[:, jj, c * 128:(c + 1) * 128], tb[t][:, 0, q, :],
                                    jj == 0, False, ['fa', 'tb%d' % t], ['ps%d' % acc[c]])
                            self.mm(self.ps[acc[c]][:, :], fb[:, jj, c * 128:(c + 1) * 128], tb[t][:, 1, q, :],
                                    False, jj == 31, ['fb', 'tb%d' % t], ['ps%d' % acc[c]])
            else:
                for q in range(2):
                    jj = 32 + q
                    for c in range(4):
                        self.mm(self.ps[acc[c]][:, 0:n], fa[:, jj, c * 128:(c + 1) * 128], self.tcx[:, q, 0, :],
                                q == 0, False, ['fa', 'tcx'], ['ps%d' % acc[c]])
                        self.mm(self.ps[acc[c]][:, 0:n], fb[:, jj, c * 128:(c + 1) * 128], self.tcx[:, q, 1, :],
                                False, q == 1, ['fb', 'tcx'], ['ps%d' % acc[c]])
            for c in range(4):
                if c % 2 == 0:
                    self.cp_act(ys[:, c, 0:n], self.ps[acc[c]][:, 0:n], ['ps%d' % acc[c]], [yk])
                else:
                    self.cp_dve(ys[:, c, 0:n], self.ps[acc[c]][:, 0:n], ['ps%d' % acc[c]], [yk])
            P.dma('sp', [(self.YB.rearrange("(c p) t -> p c t", p=128)[:, :, tok0:tok0 + n], ys[:, :, 0:n])],
                  [yk], ['YB'], yk)

    def phaseQ(self, l, blocks):
        P = self.P
        kn = self.sb("kn", [128, 4, NT], BF16)
        kp = self.sb("kp", [64, NT], BF16)
        v = self.sb("v", [128, 34, 512], BF16)
        qn = [self.sb("qn%d" % i, [128, 4, 512], BF16) for i in range(2)]
        qp = [self.sb("qp%d" % i, [64, 4, 512], BF16) for i in range(2)]
        pt = [self.sb("pt%d" % i, [128, 512], BF16) for i in range(4)]
        rl = self.sb("rl", [128, 512], F32)
        at_st = [self.sb("at_st%d" % i, [128, 4, 512], BF16) for i in range(2)]
        P.dma('sp', [(kn[:, h, :], self.KnT[h * 128:(h + 1) * 128, :]) for h in range(4)], ['KnT'], ['kn'], 'kn')
        P.dma('sp', [(kp[:], self.KpT)], ['KpT'], ['kp'], 'kp')
        P.dma('sp', [(v[:, q * 17:(q + 1) * 17, :], self.Vd[q * 17 * 128:(q + 1) * 17 * 128, :].rearrange("(s p) d -> p s d", p=128))
                     for q in range(2)], ['Vd'], ['v'], 'v')
        LOOK = 2
        seq = []
        for bi, (tok0, n, j) in enumerate(blocks):
            kts = list(range(34)) if not j else [32, 33]
            for h in range(4):
                for ki, kt in enumerate(kts):
                    seq.append((bi, h, ki, kt, len(kts)))
        ptb = {}
        grp = {}
        psum4 = [self.sb("psum4_%d" % i, [128, 512], BF16) for i in range(2)]

        def stage_s(idx):
            bi, h, ki, kt, nk = seq[idx]
            tok0, n, j = blocks[bi]
            qb = bi % 2
            if h == 0 and ki == 0:
                P.dma('sp', [(qn[qb][:, :, 0:n], self.QT[0:512, :].rearrange("(h p) t -> p h t", p=128)[:, :, tok0:tok0 + n]),
                             (qp[qb][:, :, 0:n], self.QT[512:768, :].rearrange("(h p) t -> p h t", p=64)[:, :, tok0:tok0 + n])],
                      ['QT'], ['q%d' % qb], 'q%d' % qb)
            sbk = self.ps_next(pool=(0, 1, 2))
            self.mm(self.ps[sbk][:, 0:n], kn[:, h, kt * 128:(kt + 1) * 128], qn[qb][:, h, 0:n], True, False,
                    ['kn', 'q%d' % qb], ['ps%d' % sbk])
            self.mm(self.ps[sbk][:, 0:n], kp[:, kt * 128:(kt + 1) * 128], qp[qb][:, h, 0:n], False, True,
                    ['kp', 'q%d' % qb], ['ps%d' % sbk])
            p_ = idx % 4
            ptb[idx] = p_
            self.act(pt[p_][:, 0:n], self.ps[sbk][:, 0:n], AF.Exp, ['ps%d' % sbk], ['pt%d' % p_], scale=SCALE)

        def stage_pv(idx):
            bi, h, ki, kt, nk = seq[idx]
            tok0, n, j = blocks[bi]
            qb = bi % 2
            ats = at_st[qb]
            O, Lb = (3, 4) if (bi * 4 + h) % 2 == 0 else (5, 6)
            p_ = ptb.pop(idx)
            self.mm(self.ps[O][:, 0:n], v[:, kt, h * 128:(h + 1) * 128], pt[p_][:, 0:n], ki == 0, ki == nk - 1,
                    ['v', 'pt%d' % p_], ['ps%d' % O])
            g4 = ki % 4
            lastk = (ki == nk - 1)
            if g4 == 0:
                grp['first'] = p_
                grp['src'] = (pt[p_], 'pt%d' % p_)
            elif g4 == 1:
                grp['si'] = 1 - grp.get('si', 0)
                sb_, sk_ = psum4[grp['si']], 'psum4_%d' % grp['si']
                f_ = grp['first']
                self.tt(sb_[:, 0:n], pt[f_][:, 0:n], pt[p_][:, 0:n], ALU.add, ['pt%d' % f_, 'pt%d' % p_], [sk_])
                grp['src'] = (sb_, sk_)
            else:
                sb_, sk_ = grp['src']
                self.tt(sb_[:, 0:n], sb_[:, 0:n], pt[p_][:, 0:n], ALU.add, [sk_, 'pt%d' % p_], [sk_])
            if g4 == 3 or lastk:
                sb_, sk_ = grp['src']
                self.mm(self.ps[Lb][:, 0:n], self.ones_b[:], sb_[:, 0:n], ki < 4, lastk,
                        ['ones_b', sk_], ['ps%d' % Lb])
            if ki == nk - 1:
                self.P.add('dve', lambda e, n=n, Lb=Lb: e.reciprocal(out=rl[:, 0:n], in_=self.ps[Lb][:, 0:n]), ['ps%d' % Lb], ['rl'])
                self.tt(ats[:, h, 0:n], self.ps[O][:, 0:n], rl[:, 0:n], ALU.mult, ['ps%d' % O, 'rl'], ['at_st%d' % qb])
                if h == 3:
                    P.dma('sp', [(self.AT.rearrange("(h p) t -> p h t", p=128)[:, :, tok0:tok0 + n], ats[:, :, 0:n])],
                          ['at_st%d' % qb], ['AT'], 'at_st%d' % qb)

        for idx in range(len(seq) + LOOK):
            if idx < len(seq):
                stage_s(idx)
            if idx - LOOK >= 0:
                stage_pv(idx - LOOK)

    def phaseB(self, l, blocks, src_x, src_c, last):
        P = self.P
        w = self.w[l]
        vl = V_L + l * VL_N
        vecs = self.vecs
        ureg = self.sb("ureg", [128, 24576], BF16)
        u0 = self.last_off
        osbs = [self.sb("osb%d" % i, [128, 4, D], F32, at=u0 + 16384 * i) for i in range(2)]
        hxTs = [self.sb("hxT%d" % i, [128, 16, 512], BF16, at=u0 + 32768 * i) for i in range(2)]
        ys = [self.sb("ys%d" % g, [128, 4, 512], BF16, at=u0 + 16384 + g * 4096) for g in range(4)]
        self.alloc_hx(hxTs=hxTs)
        srcs = {0: src_x, 1: src_c}
        start_block, step, finish = self.hx_pipeline(blocks, srcs)
        Wb = [self.sb("Wb%d" % i, [128, 4, 512], BF16) for i in range(2)]
        yb = self.sb("yb", [128, 4, 512], BF16)
        at = self.sb("at", [128, 4, 512], BF16)
        gt = self.sb("gt", [128, 4, 514], BF16)
        sg = [self.sb("sg%d" % i, [128, 512], F32) for i in range(3)]
        tmp = [self.sb("btmp%d" % i, [128, 512], F32) for i in range(3)]
        acc = [self.sb("bacc%d" % i, [128, 512], F32) for i in range(4)]
        mT = self.sb("mT", [128, 16, 512], BF16)
        vc = self.sb("vc", [128, 4, 512], BF16)
        vf = self.sb("vf", [128, 512], F32)
        mix = self.sb("mix", [128, 4, 512], F32)
        st6 = self.sb("st6", [128, 8], F32)
        oss = self.sb("oss", [128, 24], F32)
        sgn = 0
        tmn = 0
        wbn = 0
        GTv = self.GT.rearrange("(c p) t -> p c t", p=128)
        for bi, (tok0, n, j) in enumerate(blocks):
            src = src_c if j else src_x
            r0 = 0 if j else tok0
            ns = n // 128
            start_block(bi)
            osb = osbs[bi % 2]
            UK = ['hxT%d' % (bi % 2), 'ys0', 'ys1', 'ys2', 'ys3']
            P.dma('sp', [(yb[:, :, 0:n], self.YB.rearrange("(c p) t -> p c t", p=128)[:, :, tok0:tok0 + n])], ['YB'], ['yb'], 'yb')
            P.dma('sp', [(at[:, :, 0:n], self.AT.rearrange("(c p) t -> p c t", p=128)[:, :, tok0:tok0 + n])], ['AT'], ['at'], 'at')
            if j:
                lidx, ridx, lm, rm = SEQ, SEQ, 3, 3
            else:
                k = tok0 // 512
                lidx = (tok0 - 1) % SEQ
                ridx = (tok0 + 512) % SEQ
                lm = 0 if k == 0 else (1 if k == 4 else 2)
                rm = 1 if k == 3 else (0 if k == 7 else 2)
            P.dma('sp', [(gt[:, :, 1:n + 1], GTv[:, :, tok0:tok0 + n])], ['GT'], ['gt'], 'gt')
            P.dma('sp', [(gt[:, :, 0:1], GTv[:, :, lidx:lidx + 1]),
                         (gt[:, :, n + 1:n + 2], GTv[:, :, ridx:ridx + 1])], ['GT'], ['gt'], 'gt',
                  allow_slow_non_contiguous=True)
            self.ts(gt[:, :, 0:1], gt[:, :, 0:1], vecs[:, V_MASK + lm:V_MASK + lm + 1], None, ALU.mult, None, ['gt', 'vecs'], ['gt'])
            self.ts(gt[:, :, n + 1:n + 2], gt[:, :, n + 1:n + 2], vecs[:, V_MASK + rm:V_MASK + rm + 1], None, ALU.mult, None,
                    ['gt', 'vecs'], ['gt'])

            def wcol(name):
                return self.wload(l, 'win', 2 + (COL[name] - 832) // 512)

            def silu_chunk(W, wk, c):
                nonlocal sgn
                b = self.ps_next()
                self.proj_fm(W, wk, c * 128, 128, n, b)
                s_ = sgn % 3
                sgn += 1
                self.act(sg[s_][:, 0:n], self.ps[b][:, 0:n], AF.Silu, ['ps%d' % b], ['sg%d' % s_])
                return s_

            for (zname, other, okey, g) in (('zA', at, 'at', 0), ('zB', yb, 'yb', 1)):
                W, wk = wcol(zname)
                for c in range(4):
                    s_ = silu_chunk(W, wk, c)
                    self.tt(ys[g][:, c, 0:n], sg[s_][:, 0:n], other[:, c, 0:n], ALU.mult, ['sg%d' % s_, okey], ['ys%d' % g])
            step(bi)
            Wz, wkz = wcol('zC')
            Wc, wkc = wcol('bC')
            for c in range(4):
                s_ = silu_chunk(Wz, wkz, c)
                t_ = tmn % 3
                tmn += 1
                T = tmp[t_]
                tk = 'btmp%d' % t_
                cw = [vecs[:, vl + VL_CW + k_ * 4 + c:vl + VL_CW + k_ * 4 + c + 1] for k_ in range(3)]
                self.ts(T[:, 0:n], gt[:, c, 0:n], cw[0], None, ALU.mult, None, ['gt', 'vecs'], [tk])
                self.stt(T[:, 0:n], gt[:, c, 1:n + 1], cw[1], T[:, 0:n], ALU.mult, ALU.add, ['gt', 'vecs', tk], [tk])
                self.stt(T[:, 0:n], gt[:, c, 2:n + 2], cw[2], T[:, 0:n], ALU.mult, ALU.add, ['gt', 'vecs', tk], [tk])
                b2 = self.ps_next()
                self.proj_fm(Wc, wkc, c * 128, 128, n, b2)
                self.stt(T[:, 0:n], T[:, 0:n], vecs[:, vl + VL_CB + c:vl + VL_CB + c + 1], self.ps[b2][:, 0:n], ALU.add, ALU.mult,
                         [tk, 'vecs', 'ps%d' % b2], [tk])
                self.tt(ys[2][:, c, 0:n], T[:, 0:n], sg[s_][:, 0:n], ALU.mult, [tk, 'sg%d' % s_], ['ys2'])
            step(bi)
            Wv, wkv = wcol('vD')
            for s in range(ns):
                b = self.ps_next()
                for k in range(16):
                    self.mm(self.ps[b][:, :], self.hxT[:, k, s * 128:(s + 1) * 128], Wv[:, k, 0:512], k == 0, k == 15,
                            [wkv, self.hk], ['ps%d' % b])
                self.act(vf[:, :], self.ps[b][:, :], AF.Identity, ['ps%d' % b], ['vf', 'st6a'], accum_out=st6[:, 0:1])
                self.act(self.jk[:, :], self.ps[b][:, :], AF.Square, ['ps%d' % b], ['jk', 'st6b'], accum_out=st6[:, 1:2])
                self.ts(st6[:, 2:3], st6[:, 0:1], 1.0 / 512, None, ALU.mult, None, ['st6a'], ['st6c'])
                self.tt(st6[:, 3:4], st6[:, 2:3], st6[:, 2:3], ALU.mult, ['st6c'], ['st6d'])
                self.stt(st6[:, 4:5], st6[:, 1:2], 1.0 / 512, st6[:, 3:4], ALU.mult, ALU.subtract, ['st6b', 'st6d'], ['st6e'])
                self.ts(st6[:, 5:6], st6[:, 4:5], EPS, None, ALU.add, None, ['st6e'], ['st6f'])
                self.act(st6[:, 5:6], st6[:, 5:6], AF.Sqrt, ['st6f'], ['st6f'])
                self.P.add('dve', lambda e: e.reciprocal(out=st6[:, 5:6], in_=st6[:, 5:6]), ['st6f'], ['st6f'])
                self.ts(vf[:, :], vf[:, :], st6[:, 2:3], st6[:, 5:6], ALU.subtract, ALU.mult, ['vf', 'st6c', 'st6f'], ['vf'])
                self.tt(vf[:, :], vf[:, :], self.lnb[:, 0:512], ALU.mult, ['vf', 'lnb'], ['vf'])
                self.tt(vc[:, s, :], vf[:, :], self.lnb[:, 512:1024], ALU.add, ['vf', 'lnb'], ['vc'])
            for g in range(4):
                b = self.ps_next()
                for s in range(ns):
                    self.mm(self.ps[b][:, s * 128:(s + 1) * 128], vc[:, s, g * 128:(g + 1) * 128], self.WsT[:, g, :], True, True,
                            ['vc', 'WsT'], ['ps%d' % b])
                for s in range(ns):
                    self.tt(mix[:, g, s * 128:(s + 1) * 128], self.ps[b][:, s * 128:(s + 1) * 128],
                            self.lnb[:, 1024 + g * 128:1024 + (g + 1) * 128], ALU.add, ['ps%d' % b, 'lnb'], ['mix'])
            Wu, wku = wcol('uD')
            for c in range(4):
                b = self.ps_next()
                self.proj_fm(Wu, wku, c * 128, 128, n, b)
                self.tt(mix[:, c, 0:n], self.ps[b][:, 0:n], mix[:, c, 0:n], ALU.mult, ['ps%d' % b, 'mix'], ['mix'])
            Wz, wkz = wcol('zD')
            for c in range(4):
                s_ = silu_chunk(Wz, wkz, c)
                self.tt(ys[3][:, c, 0:n], mix[:, c, 0:n], sg[s_][:, 0:n], ALU.mult, ['mix', 'sg%d' % s_], ['ys3'])
            step(bi)
            for jd in range(4):
                if jd in (1, 2):
                    step(bi)
                for g in range(4):
                    W, wk = self.wload(l, 'win', 12 + 4 * g + jd)
                    wb = wbn % 2
                    wbn += 1
                    if self.conv.get((l, 'br')):
                        P.dma('pool', [(Wb[wb][:], self.CW[l]['br'][g * 4 + jd])], ['cv_br%d' % l], ['Wb%d' % wb], 'Wb%d' % wb)
                    else:
                        P.dma('pool', [(Wb[wb][:], w['br'][g, :, jd * 512:(jd + 1) * 512].rearrange("(k p) c -> p k c", p=128))],
                              [], ['Wb%d' % wb], 'Wb%d' % wb)
                    for dc in range(4):
                        b = self.ps_next()
                        self.proj_fm(W, wk, dc * 128, 128, n, b)
                        s_ = sgn % 3
                        sgn += 1
                        self.act(sg[s_][:, 0:n], self.ps[b][:, 0:n], AF.Sigmoid, ['ps%d' % b], ['sg%d' % s_])
                        b2 = self.ps_next()
                        for k in range(4):
                            self.mm(self.ps[b2][:, 0:n], Wb[wb][:, k, dc * 128:(dc + 1) * 128], ys[g][:, k, 0:n], k == 0, k == 3,
                                    ['Wb%d' % wb, 'ys%d' % g], ['ps%d' % b2])
                        A_ = acc[dc]
                        ak = 'bacc%d' % dc
                        sk = 'sg%d' % s_
                        if g == 0:
                            self.tt(A_[:, 0:n], sg[s_][:, 0:n], self.ps[b2][:, 0:n], ALU.mult, [sk, 'ps%d' % b2], [ak])
                        else:
                            self.tt(sg[s_][:, 0:n], sg[s_][:, 0:n], self.ps[b2][:, 0:n], ALU.mult, [sk, 'ps%d' % b2], [sk])
                            if g < 3:
                                self.tt(A_[:, 0:n], A_[:, 0:n], sg[s_][:, 0:n], ALU.add, [ak, sk], [ak])
                            else:
                                self.tt(mT[:, jd * 4 + dc, 0:n], A_[:, 0:n], sg[s_][:, 0:n], ALU.add, [ak, sk], ['mT'])
            finish(bi)
            for jo in range(4):
                W, wk = self.wload(l, 'out', jo)
                for s in range(ns):
                    b = self.ps_next()
                    for k in range(16):
                        self.mm(self.ps[b][:, :], mT[:, k, s * 128:(s + 1) * 128], W[:, k, 0:512], k == 0, k == 15,
                                [wk, 'mT'], ['ps%d' % b])
                    self.cp_act(osb[:, s, jo * 512:(jo + 1) * 512], self.ps[b][:, :], ['ps%d' % b], ['osb%d' % s] + UK)
                    self.act(self.jk[:, :], self.ps[b][:, :], AF.Square, ['ps%d' % b], ['jk', 'oss%d_%d' % (s, jo)],
                             accum_out=oss[:, s * 4 + jo:s * 4 + jo + 1])
            for s in range(ns):
                self.P.add('dve', lambda e, s=s: e.reduce_sum(out=oss[:, 16 + s:17 + s], in_=oss[:, s * 4:s * 4 + 4],
                                                              axis=mybir.AxisListType.X),
                           ['oss%d_%d' % (s, q) for q in range(4)], ['orr%d' % s])
                self.rsqrt_mean(oss[:, 20 + s:21 + s], oss[:, 16 + s:17 + s], D, ['orr%d' % s], ['orq%d' % s])
                xb = s % 2
                xrb = self.xs[xb]
                xk = 'xs%d' % xb
                rr0 = r0 + s * 128
                P.dma('sp', [(xrb[:], src[rr0:rr0 + 128, :])], ['xsrc'], [xk], xk)
                self.stt(osb[:, s, :], osb[:, s, :], oss[:, 20 + s:21 + s], self.GPb[:, j, :], ALU.mult, ALU.mult,
                         ['osb%d' % s, 'orq%d' % s, 'GPb'] + UK, ['osb%d' % s] + UK)
                self.tt(xrb[:], xrb[:], osb[:, s, :], ALU.add, [xk, 'osb%d' % s] + UK, [xk])
                if j:
                    dst, dk = self.C1[rr0:rr0 + 128, :], 'xdst'
                elif last:
                    dst, dk = self.y[rr0:rr0 + 128, :], 'xdst'
                else:
                    dst, dk = self.X1[rr0:rr0 + 128, :], 'xdst'
                P.dma('sp', [(dst, xrb[:])], [xk], [dk], xk)


def _fm(v, n):
    return np.ascontiguousarray(np.asarray(v, np.float32).reshape(n, 128).T)


def prepare_core_inputs(inputs, core, layers=(0, 1)):
    b, r = core // 2, core % 2
    f32 = np.float32
    perm = (np.arange(SEQ) + HALF * r) % SEQ
    d = {}
    d['x'] = np.ascontiguousarray(np.asarray(inputs['x'])[b][perm])
    d['ctx'] = np.ascontiguousarray(np.asarray(inputs['ctx'])[b])
    vecs = np.zeros((128, NV), f32)
    vecs[:, V_C:V_C + 16] = _fm(inputs['c'][b], 16)
    vecs[:, V_C + 16:V_C + 32] = _fm(inputs['c_ctx'], 16)
    for l in range(L):
        o = V_L + l * VL_N
        vecs[:, o + VL_BMOD:o + VL_BMOD + 48] = _fm(inputs['b_mod'][l], 48)
        vecs[:, o + VL_PREG:o + VL_PREG + 16] = _fm(inputs['pre_g'][l], 16)
        vecs[:, o + VL_POSTG:o + VL_POSTG + 16] = _fm(inputs['post_g'][l], 16)
        vecs[:, o + VL_QG:o + VL_QG + 4] = _fm(inputs['q_norm_g'][l], 4)
        vecs[:, o + VL_KVG:o + VL_KVG + 2] = _fm(inputs['kv_norm_g'][l], 2)
        for k in range(3):
            vecs[:, o + VL_CW + 4 * k:o + VL_CW + 4 * k + 4] = _fm(inputs['conv_w'][l][k], 4)
        vecs[:, o + VL_CB:o + VL_CB + 4] = _fm(inputs['conv_b'][l], 4)
    vecs[:, V_MASK + 0] = 1.0 if r == 1 else 0.0
    vecs[:, V_MASK + 1] = 1.0 if r == 0 else 0.0
    vecs[:, V_MASK + 2] = 1.0
    vecs[:, V_MASK + 3] = 0.0
    d['vecs'] = vecs
    bct = np.zeros((128, NB), f32)
    for l in range(L):
        bct[:, l * 1536:l * 1536 + 512] = np.asarray(inputs['sgu_ln_g'][l], f32)[None, :]
        bct[:, l * 1536 + 512:l * 1536 + 1024] = np.asarray(inputs['sgu_ln_b'][l], f32)[None, :]
        bct[:, l * 1536 + 1024:l * 1536 + 1536] = np.asarray(inputs['sgu_b'][l], f32).reshape(1, 512)
    d['bct'] = bct
    d['ident'] = np.eye(128, dtype=f32)
    nat = perm.astype(np.float64)
    rows, cols = np.floor(nat / 64), nat % 64
    inv = 10000.0 ** (-np.arange(0, 32, 2, dtype=np.float64) / 32)
    ang = np.concatenate([rows[:, None] * inv, cols[:, None] * inv], -1)
    ang = (np.concatenate([rows[:, None].astype(f32) * inv.astype(f32), cols[:, None].astype(f32) * inv.astype(f32)], -1)).astype(f32)
    cos, sin = np.cos(ang).astype(f32), np.sin(ang).astype(f32)
    rope = np.zeros((64, 2, NT), f32)
    rope[0:32, 0, :SEQ] = cos.T
    rope[32:64, 0, :SEQ] = cos.T
    rope[0:32, 1, :SEQ] = sin.T
    rope[32:64, 1, :SEQ] = sin.T
    rope[:, 0, SEQ:] = 1.0
    d['ropeT'] = rope
    cc = np.arange(128, dtype=np.float64)
    a128 = 2 * np.pi * np.outer(cc, cc) / 128
    d['c128s'] = np.concatenate([np.cos(a128), np.sin(a128)], 1).astype(f32) / np.sqrt(128).astype(f32)
    kk = (perm[:, None].astype(np.int64) * perm[None, :].astype(np.int64)) % SEQ
    a4 = 2 * np.pi * np.arange(SEQ, dtype=np.float64) / SEQ
    d['tcf'] = (np.cos(a4) / 64.0).astype(f32)[kk]
    d['tsf'] = (-np.sin(a4) / 64.0).astype(f32)[kk]
    n256 = np.arange(CTX, dtype=np.float64)
    ac = 2 * np.pi * np.outer(n256, n256) / CTX
    tcx = np.zeros((CTX, 2, CTX), f32)
    tcx[:, 0, :] = np.cos(ac) / 16.0
    tcx[:, 1, :] = -np.sin(ac) / 16.0
    d['tcx'] = tcx
    for l in layers:
        d['w_mod%d' % l] = np.asarray(inputs['w_mod'][l], f32)
        d['w_in%d' % l] = np.asarray(inputs['w_in'][l], f32)
        d['w_uq%d' % l] = np.asarray(inputs['w_uq'][l], f32)
        d['w_ukv%d' % l] = np.asarray(inputs['w_ukv'][l], f32)
        d['sgu_w%d' % l] = np.asarray(inputs['sgu_w'][l], f32)
        d['w_branch%d' % l] = np.asarray(inputs['w_branch'][l], f32)
        d['w_out%d' % l] = np.asarray(inputs['w_out'][l], f32)
    return d


def kernel(**inputs):
    inputs = {k: np.asarray(v) for k, v in inputs.items()}
    nc = Builder().build()
    in_maps = [prepare_core_inputs(inputs, c) for c in range(8)]
    res = run_bass_kernel_spmd(nc, in_maps, core_ids=list(range(8)))
    out = np.zeros((4, SEQ, D), np.float32)
    for c in range(8):
        b, r = c // 2, c % 2
        out[b, HALF * r:HALF * (r + 1), :] = res.results[c]["y"]
    return out
```

```python
import os
import collections
import numpy as np
import concourse.bass as bass
import concourse.mybir as mybir
from concourse.bass_utils import run_bass_kernel_spmd

F32 = mybir.dt.float32
BF16 = mybir.dt.bfloat16
AF = mybir.ActivationFunctionType
ALU = mybir.AluOpType

D = 2048
SEQ = 4096
CTX = 256
NT = SEQ + CTX
HALF = SEQ // 2
L = 2
INW = 14144
EPS = 1e-6
SCALE = 192.0 ** -0.5
COL = {n: 832 + 512 * i for i, n in enumerate(
    ['zA', 'uB', 'zB', 'xC', 'bC', 'cC', 'zC', 'uD', 'vD', 'zD'])}
GATE0 = 5952

V_C = 0
V_L = 32
VL_BMOD, VL_PREG, VL_POSTG, VL_QG, VL_KVG, VL_CW, VL_CB = 0, 48, 64, 80, 84, 86, 98
VL_N = 102
V_MASK = V_L + L * VL_N
NV = V_MASK + 4
NB = L * 1536


class Prog:
    ENGS = ('pe', 'act', 'dve', 'pool', 'sp')

    def __init__(self, nc):
        self.nc = nc
        self.q = {e: [] for e in self.ENGS}
        self.res = {}
        self.dcount = {}
        self.lastc = {e: -1 for e in ('pe', 'act', 'dve', 'pool')}

    def _deps(self, eng, reads, writes, is_dma):
        ev_e = {}
        ev_d = {}

        def addev(ev, kind):
            if ev is None:
                return
            if ev[0] == 'e':
                if ev[1] == eng and not is_dma:
                    if eng == 'pe' or kind != 'raw':
                        return
                if ev_e.get(ev[1], -1) < ev[2]:
                    ev_e[ev[1]] = ev[2]
            else:
                n_ = self.dcount[ev[1]] if ev[1].startswith('cv_') else ev[2]
                if ev_d.get(ev[1], 0) < n_:
                    ev_d[ev[1]] = n_
        for r in reads:
            st = self.res.get(r)
            if st:
                addev(st['w'], 'raw')
        for w in writes:
            st = self.res.get(w)
            if st:
                addev(st['w'], 'waw')
                for e2, i2 in st['re'].items():
                    addev(('e', e2, i2), 'war')
                for k2, n2 in st['rd'].items():
                    addev(('d', k2, n2), 'war')
        return ev_e, ev_d

    def _update(self, ev, reads, writes):
        for w in writes:
            self.res[w] = {'w': ev, 're': {}, 'rd': {}}
        for r in reads:
            st = self.res.setdefault(r, {'w': None, 're': {}, 'rd': {}})
            if ev[0] == 'e':
                if st['re'].get(ev[1], -1) < ev[2]:
                    st['re'][ev[1]] = ev[2]
            else:
                if st['rd'].get(ev[1], 0) < ev[2]:
                    st['rd'][ev[1]] = ev[2]

    def add(self, eng, fn, reads=(), writes=()):
        ev_e, ev_d = self._deps(eng, reads, writes, False)
        idx = len(self.q[eng])
        self.q[eng].append({'fn': fn, 'we': ev_e, 'wd': ev_d, 'dma': None})
        self.lastc[eng] = idx
        self._update(('e', eng, idx), reads, writes)

    def dma(self, eng, pairs, reads, writes, semkey, throttle=None, **kw):
        ev_e, ev_d = self._deps(eng, reads, writes, True)
        n = self.dcount.get(semkey, 0)
        if throttle is not None and n - throttle > 0:
            ev_d[semkey] = max(ev_d.get(semkey, 0), n - throttle)
        for i, (o, a) in enumerate(pairs):
            n += 1
            self.q[eng].append({
                'fn': (lambda e, o=o, a=a: e.dma_start(out=o, in_=a, **kw)),
                'we': ev_e if i == 0 else {}, 'wd': ev_d if i == 0 else {}, 'dma': semkey})
        self.dcount[semkey] = n
        self._update(('d', semkey, n), reads, writes)

    def barrier(self, final=False):
        last = dict(self.lastc)
        for e in self.ENGS:
            we = {e2: i for e2, i in last.items() if i >= 0 and e2 != e}
            wd = dict(self.dcount) if final else {k: n for k, n in self.dcount.items() if not k.startswith('cv_')}
            self.q[e].append({'fn': None, 'we': we, 'wd': wd, 'dma': None})
        self.res = {k: v for k, v in self.res.items() if k.startswith('cv_') and not final}

    def emit(self):
        nc = self.nc
        needed = {e: set() for e in self.ENGS}
        for e in self.ENGS:
            for op in self.q[e]:
                for e2, i2 in op['we'].items():
                    needed[e2].add(i2)
        esem = {e: nc.alloc_semaphore('s_' + e) for e in ('pe', 'act', 'dve', 'pool')}
        dsem = {k: nc.alloc_semaphore('d_%d' % i) for i, k in enumerate(sorted(self.dcount))}
        val = {}
        for e in ('pe', 'act', 'dve', 'pool'):
            c = 0
            for i, op in enumerate(self.q[e]):
                if i in needed[e]:
                    c += 1
                    val[(e, i)] = c
        q, dcount = self.q, self.dcount

        def run(ename, eng):
            waited = {}
            for i, op in enumerate(q[ename]):
                for e2, i2 in op['we'].items():
                    v = val[(e2, i2)]
                    if waited.get(('e', e2), 0) < v:
                        eng.wait_ge(esem[e2], v)
                        waited[('e', e2)] = v
                for k2, n2 in op['wd'].items():
                    v = 16 * n2
                    if waited.get(('d', k2), 0) < v:
                        eng.wait_ge(dsem[k2], v)
                        waited[('d', k2)] = v
                if op['fn'] is None:
                    continue
                ins = op['fn'](eng)
                if op['dma'] is not None:
                    ins.then_inc(dsem[op['dma']], 16)
                elif i in needed[ename]:
                    ins.then_inc(esem[ename], 1)
            if ename == 'sp':
                for k2, n2 in dcount.items():
                    if waited.get(('d', k2), 0) < 16 * n2:
                        eng.wait_ge(dsem[k2], 16 * n2)

        with nc.Block() as block:
            @block.tensor
            def _(e):
                run('pe', e)

            @block.scalar
            def _(e):
                run('act', e)

            @block.vector
            def _(e):
                run('dve', e)

            @block.gpsimd
            def _(e):
                run('pool', e)

            @block.sync
            def _(e):
                run('sp', e)


class Builder:
    def __init__(self, layers=(0, 1), debug=False, phases='SAQFB', blocks_limit=None):
        self.layers = layers
        self.debug = debug
        self.phases = phases
        self.blocks_limit = blocks_limit
        nc = self.nc = bass.Bass("TRN2", target_bir_lowering=False)
        self.P = Prog(nc)

        def inp(name, shape):
            return nc.dram_tensor(name, list(shape), F32, kind="ExternalInput").ap()
        self.x = inp("x", [SEQ, D])
        self.ctx = inp("ctx", [CTX, D])
        self.vecs_d = inp("vecs", [128, NV])
        self.bct_d = inp("bct", [128, NB])
        self.ident_d = inp("ident", [128, 128])
        self.rope_d = inp("ropeT", [64, 2, NT])
        self.c128_d = inp("c128s", [128, 256])
        self.tcf_d = inp("tcf", [SEQ, SEQ])
        self.tsf_d = inp("tsf", [SEQ, SEQ])
        self.tcx_d = inp("tcx", [CTX, 2, CTX])
        self.w = {}
        for l in layers:
            self.w[l] = dict(
                mod=inp("w_mod%d" % l, [D, 3 * D]), win=inp("w_in%d" % l, [D, INW]),
                uq=inp("w_uq%d" % l, [512, 768]), ukv=inp("w_ukv%d" % l, [256, 1024]),
                sgu=inp("sgu_w%d" % l, [4, 128, 128]), br=inp("w_branch%d" % l, [4, 512, D]),
                out=inp("w_out%d" % l, [D, D]))
        self.y = nc.dram_tensor("y", [HALF, D], F32, kind="ExternalOutput").ap()
        kind = "ExternalOutput" if debug else "Internal"

        def scr(name, shape, dt=BF16):
            return nc.dram_tensor(name, list(shape), dt, kind=kind).ap()
        self.KnT = scr("KnT", [512, NT])
        self.KpT = scr("KpT", [64, NT])
        self.Vd = scr("Vd", [NT, 512])
        self.FA = scr("FA", [NT, 512])
        self.FB = scr("FB", [NT, 512])
        self.QT = scr("QT", [768, NT])
        self.GT = scr("GT", [512, NT])
        self.YB = scr("YB", [512, NT])
        self.AT = scr("AT", [512, NT])
        self.X1 = scr("X1", [SEQ, D], F32)
        self.C1 = scr("C1", [CTX, D], F32)
        self.TC = nc.dram_tensor("TC", [SEQ, SEQ], BF16, kind="Internal").ap()
        self.TS = nc.dram_tensor("TS", [SEQ, SEQ], BF16, kind="Internal").ap()
        self.CW = {}
        for l in layers:
            self.CW[l] = dict(
                win=nc.dram_tensor("cw_in%d" % l, [28, 128, 16, 512], BF16, kind="Internal").ap(),
                br=nc.dram_tensor("cw_br%d" % l, [16, 128, 4, 512], BF16, kind="Internal").ap(),
                out=nc.dram_tensor("cw_out%d" % l, [4, 128, 16, 512], BF16, kind="Internal").ap(),
                mod=nc.dram_tensor("cw_mod%d" % l, [12, 128, 16, 512], BF16, kind="Internal").ap())
        self.conv = {}
        self.bg_tasks = collections.deque()
        self.psn = 0
        self.wn = 0
        self.ncnt = 0
        self.off = (nc.sbuf_base + 63) // 64 * 64
        self.sb_top = nc.sbuf_top

    def sb(self, name, shape, dt, at=None):
        nbytes = int(np.prod(shape[1:])) * (4 if dt == F32 else 2)
        if at is None:
            off = (self.off + 63) // 64 * 64
            self.off = off + nbytes
            assert self.off <= self.sb_top, ("SBUF overflow", name, self.off)
        else:
            off = at
        self.ncnt += 1
        t = self.nc.alloc_sbuf_tensor_at("%s_%d" % (name, self.ncnt), list(shape), dt, offset=off)
        self.last_off = off
        return t

    def ps_next(self, pool=(0, 1, 2, 3, 4, 5)):
        b = pool[self.psn % len(pool)]
        self.psn += 1
        return b

    def act(self, out, in_, func, reads, writes, bias=None, scale=None, accum_out=None):
        kw = {}
        if bias is not None:
            kw['bias'] = bias
        if scale is not None:
            kw['scale'] = scale
        if accum_out is not None:
            kw['accum_out'] = accum_out
        self.P.add('act', lambda e: e.activation(out=out, in_=in_, func=func, **kw), reads, writes)

    def tt(self, out, in0, in1, op, reads, writes, eng='dve'):
        self.P.add(eng, lambda e: e.tensor_tensor(out=out, in0=in0, in1=in1, op=op), reads, writes)

    def ts(self, out, in0, s1, s2, op0, op1, reads, writes, eng='dve'):
        if s2 is None:
            self.P.add(eng, lambda e: e.tensor_scalar(out=out, in0=in0, scalar1=s1, scalar2=None,
                                                      op0=op0), reads, writes)
        else:
            self.P.add(eng, lambda e: e.tensor_scalar(out=out, in0=in0, scalar1=s1, scalar2=s2,
                                                      op0=op0, op1=op1), reads, writes)

    def stt(self, out, in0, s, in1, op0, op1, reads, writes, eng='dve'):
        self.P.add(eng, lambda e: e.scalar_tensor_tensor(out=out, in0=in0, scalar=s, in1=in1,
                                                         op0=op0, op1=op1), reads, writes)

    def mm(self, out, lhsT, rhs, start, stop, reads, writes):
        self.P.add('pe', lambda e: e.matmul(out, lhsT, rhs, start=start, stop=stop), reads, writes)

    def rsqrt_mean(self, out, in_, n, reads, writes):
        self.ts(out, in_, 1.0 / n, EPS, ALU.mult, ALU.add, reads, writes)
        self.act(out, out, AF.Sqrt, writes, writes)
        self.P.add('dve', lambda e: e.reciprocal(out=out, in_=out), writes, writes)

    @staticmethod
    def in_entry(e):
        return (0, 320) if e == 0 else (320 + 512 * (e - 1), 512)

    def wload(self, l, kind, idx, pieces=None):
        b = self.wn % len(self.W)
        self.wn += 1
        if self.wn % 3 == 0 and self.bg_tasks:
            self.bg_tasks.popleft()()
        W = self.W[b]
        if kind == 'win':
            c0, width = self.in_entry(idx)
        else:
            c0, width = idx * 512, 512
        if pieces is None:
            pieces = [(0, width, 0)]
        pairs = []
        if self.conv.get((l, kind)):
            src = self.CW[l][kind][idx]
            for (o, n, d0) in pieces:
                pairs.append((W[:, :, d0:d0 + n], src[:, :, o:o + n]))
            reads = ['cv_%s%d' % (kind, l)]
        else:
            wd = self.w[l][kind]
            for (o, n, d0) in pieces:
                pairs.append((W[:, :, d0:d0 + n],
                              wd[:, c0 + o:c0 + o + n].rearrange("(k p) c -> p k c", p=128)))
            reads = []
        self.P.dma('pool', pairs, reads=reads, writes=['W%d' % b], semkey='W%d' % b)
        return W, 'W%d' % b

    def convert_weights(self, l, kinds, in_entries=None, defer=False):
        if defer:
            for kind in kinds:
                self.bg_tasks.extend(self._convert_tasks(l, kind, in_entries))
            return
        for kind in kinds:
            for t in self._convert_tasks(l, kind, in_entries):
                t()

    def _convert_tasks(self, l, kind, in_entries=None):
        P = self.P
        w = self.w[l]
        tasks = []
        key = 'cv_%s%d' % (kind, l)
        dst = self.CW[l][kind]

        def dma_task(d, s_, rk):
            return lambda: P.dma('pool', [(d, s_)], [], [rk], key, throttle=2)
        if kind == 'win':
            ents = in_entries if in_entries is not None else list(range(28))
            for e in ents:
                c0, width = self.in_entry(e)
                tasks.append(dma_task(dst[e][:, :, 0:width], w['win'][:, c0:c0 + width].rearrange("(k p) c -> p k c", p=128), '%s_%d' % (key, e)))
        elif kind == 'br':
            for g in range(4):
                for jd in range(4):
                    tasks.append(dma_task(dst[g * 4 + jd], w['br'][g, :, jd * 512:(jd + 1) * 512].rearrange("(k p) c -> p k c", p=128),
                                          '%s_%d' % (key, g * 4 + jd)))
        else:
            nch = 12 if kind == 'mod' else 4
            for ci in range(nch):
                tasks.append(dma_task(dst[ci], w[kind][:, ci * 512:(ci + 1) * 512].rearrange("(k p) c -> p k c", p=128), '%s_%d' % (key, ci)))

        def fin():
            P.res[key] = {'w': ('d', key, P.dcount[key]), 're': {}, 'rd': {}}
            self.conv[(l, kind)] = True
        tasks.append(fin)
        return tasks

    def bg_flush(self):
        while self.bg_tasks:
            self.bg_tasks.popleft()()

    def _old_convert_weights(self, l, kinds, in_entries=None):
        P = self.P
        w = self.w[l]
        for kind in kinds:
            key = 'cv_%s%d' % (kind, l)
            dst = self.CW[l][kind]
            if kind == 'win':
                ents = in_entries if in_entries is not None else list(range(28))
                for e in ents:
                    c0, width = self.in_entry(e)
                    P.dma('pool', [(dst[e][:, :, 0:width], w['win'][:, c0:c0 + width].rearrange("(k p) c -> p k c", p=128))],
                          [], ['%s_%d' % (key, e)], key)
            elif kind == 'br':
                for g in range(4):
                    for jd in range(4):
                        P.dma('pool', [(dst[g * 4 + jd], w['br'][g, :, jd * 512:(jd + 1) * 512].rearrange("(k p) c -> p k c", p=128))],
                              [], ['%s_%d' % (key, g * 4 + jd)], key)
            else:
                nch = 12 if kind == 'mod' else 4
                for ci in range(nch):
                    P.dma('pool', [(dst[ci], w[kind][:, ci * 512:(ci + 1) * 512].rearrange("(k p) c -> p k c", p=128))],
                          [], ['%s_%d' % (key, ci)], key)
            P.res[key] = {'w': ('d', key, P.dcount[key]), 're': {}, 'rd': {}}
            self.conv[(l, kind)] = True

    def convert_tables(self):
        P = self.P
        for j in range(32):
            for (dst, src) in ((self.TC, self.tcf_d), (self.TS, self.tsf_d)):
                P.dma('pool', [(dst[j * 128:(j + 1) * 128, :].rearrange("r (a b) -> r a b", b=2048),
                                src[j * 128:(j + 1) * 128, :].rearrange("r (a b) -> r a b", b=2048))],
                      [], ['cv_T_%d' % j], 'cv_T', throttle=8)
        P.res['cv_T'] = {'w': ('d', 'cv_T', P.dcount['cv_T']), 're': {}, 'rd': {}}

    def build(self):
        nc, P = self.nc, self.P
        self.vecs = self.sb("vecs", [128, NV], F32)
        self.ident_f = self.sb("ident_f", [128, 128], F32)
        self.ident_b = self.sb("ident_b", [128, 128], BF16)
        self.ones_f = self.sb("ones_f", [128, 128], F32)
        self.ones_b = self.sb("ones_b", [128, 128], BF16)
        self.c128 = self.sb("c128", [128, 256], BF16)
        self.tcx = self.sb("tcx", [128, 2, 2, CTX], BF16)
        self.modT = self.sb("modT", [128, 48, 2], F32)
        self.G = self.sb("G", [128, 16, 2], F32)
        self.GP = self.sb("GP", [128, 16, 2], F32)
        self.GPb = self.sb("GPb", [128, 2, D], F32)
        self.sc = self.sb("sc", [128, 16, 2], BF16)
        self.WsT = self.sb("WsT", [128, 4, 128], BF16)
        self.lnb = self.sb("lnb", [128, 1536], F32)
        self.jk = self.sb("jk", [128, 512], BF16)
        self.ps = [nc.alloc_psum_tensor("ps%d" % i, [128, 512], F32) for i in range(7)]
        self.pst = nc.alloc_psum_tensor("pst", [128, 1024], BF16)
        self.base_off = self.off

        self.setup0()
        self.off = self.base_off
        self.W = [self.sb("W%d" % i, [128, 16, 512], BF16) for i in range(2)]
        self.base_off = self.off
        for l in self.layers:
            last = (l == L - 1)
            src_x = self.x if l == 0 else self.X1
            src_c = self.ctx if l == 0 else self.C1
            blocks = [(k * 512, 512, 0) for k in range(8)] + [(SEQ, CTX, 1)]
            if self.blocks_limit:
                blocks = blocks[:self.blocks_limit]
            bblocks = blocks[:4] if last else blocks
            for ph in self.phases:
                self.off = self.base_off
                if ph == 'S':
                    self.setup_layer(l)
                elif ph == 'A':
                    self.phaseA(l, blocks, src_x, src_c, last)
                elif ph == 'F':
                    self.phaseF(l, bblocks)
                elif ph == 'Q':
                    if l == self.layers[0]:
                        self.convert_tables()
                        self.convert_weights(l, ['win'], in_entries=[2 + i for i in (0, 2, 6, 4, 8, 7, 9)] + list(range(12, 28)))
                        self.convert_weights(l, ['br', 'out'])
                        for l2 in self.layers[1:]:
                            self.convert_weights(l2, ['mod', 'win', 'br', 'out'], defer=True)
                    self.phaseQ(l, bblocks)
                elif ph == 'B':
                    self.phaseB(l, bblocks, src_x, src_c, last)
                    self.bg_flush()
                P.barrier()
        P.barrier(final=True)
        P.emit()
        return nc

    def setup0(self):
        P, nc = self.P, self.nc
        P.dma('sp', [(self.vecs[:], self.vecs_d)], [], ['vecs'], 'vecs')
        P.dma('sp', [(self.ident_f[:], self.ident_d)], [], ['ident_f'], 'ident_f')
        P.add('dve', lambda e: e.tensor_copy(out=self.ident_b[:], in_=self.ident_f[:]),
              ['ident_f'], ['ident_b'])
        P.add('dve', lambda e: e.memset(self.ones_f[:], 1.0), [], ['ones_f'])
        P.add('dve', lambda e: e.memset(self.ones_b[:], 1.0), [], ['ones_b'])
        P.dma('pool', [(self.c128[:], self.c128_d)], [], ['c128'], 'c128')
        P.dma('pool', [(self.tcx[:, j, :, :], self.tcx_d[j * 128:(j + 1) * 128, :, :]) for j in range(2)],
              [], ['tcx'], 'tcx')
        if self.debug:
            tch = self.sb("tch", [1, 64], F32)
            aps = [self.x, self.ctx, self.bct_d, self.c128_d, self.tcf_d, self.tsf_d]
            for l in self.layers:
                aps += [self.w[l][k] for k in ('mod', 'win', 'uq', 'ukv', 'br', 'out', 'sgu')]
            prs = []
            for i, a in enumerate(aps):
                src = a[0:1, 0:1] if len(a.shape) == 2 else a[0, 0:1, 0:1]
                prs.append((tch[0:1, i:i + 1], src))
            prs.append((tch[0:1, 40:41], self.rope_d[0:1, 0, 0:1]))
            prs.append((tch[0:1, 41:42], self.tcx_d[0:1, 0, 0:1]))
            P.dma('sp', prs, [], ['tch'], 'tch')
        P.barrier()

    def gen_tables(self):
        P = self.P
        HW = SEQ // 2
        cp = self.sb("g_cp", [128, SEQ], F32)
        sp_ = self.sb("g_sp", [128, SEQ], F32)
        cjb = [self.sb("g_cj%d" % i, [128, HW], F32) for i in range(2)]
        sjb = [self.sb("g_sj%d" % i, [128, HW], F32) for i in range(2)]
        t1 = self.sb("g_t1", [128, HW], F32)
        t2 = self.sb("g_t2", [128, HW], F32)
        oc = [self.sb("g_oc%d" % i, [128, HW], BF16) for i in range(2)]
        os_ = [self.sb("g_os%d" % i, [128, HW], BF16) for i in range(2)]
        P.dma('sp', [(cp[:], self.cp_d)], [], ['g_cp'], 'g_cp')
        P.dma('sp', [(sp_[:], self.spp_d)], [], ['g_sp'], 'g_sp')
        it = 0
        for j in range(32):
            for hf in range(2):
                b = it % 2
                it += 1
                cs = slice(hf * HW, (hf + 1) * HW)
                P.dma('sp', [(cjb[b][:], self.cj_d[j, cs].partition_broadcast(128))], [], ['g_cj%d' % b], 'g_cj%d' % b)
                P.dma('sp', [(sjb[b][:], self.sj_d[j, cs].partition_broadcast(128))], [], ['g_sj%d' % b], 'g_sj%d' % b)
                self.tt(t1[:], cp[:, cs], cjb[b][:], ALU.mult, ['g_cp', 'g_cj%d' % b], ['g_t1'])
                self.tt(t2[:], sp_[:, cs], sjb[b][:], ALU.mult, ['g_sp', 'g_sj%d' % b], ['g_t2'], eng='pool')
                self.tt(oc[b][:], t1[:], t2[:], ALU.subtract, ['g_t1', 'g_t2'], ['g_oc%d' % b])
                P.dma('sp', [(self.TC[j * 128:(j + 1) * 128, cs], oc[b][:])], ['g_oc%d' % b], ['TC'], 'g_oc%d' % b)
                self.tt(t1[:], sp_[:, cs], cjb[b][:], ALU.mult, ['g_sp', 'g_cj%d' % b], ['g_t1'])
                self.tt(t2[:], cp[:, cs], sjb[b][:], ALU.mult, ['g_cp', 'g_sj%d' % b], ['g_t2'], eng='pool')
                self.stt(os_[b][:], t1[:], -1.0, t2[:], ALU.mult, ALU.subtract, ['g_t1', 'g_t2'], ['g_os%d' % b])
                P.dma('sp', [(self.TS[j * 128:(j + 1) * 128, cs], os_[b][:])], ['g_os%d' % b], ['TS'], 'g_os%d' % b)

    def setup_layer(self, l):
        P = self.P
        w = self.w[l]
        vl = V_L + l * VL_N
        vecs = self.vecs
        for j in range(2):
            self.act(self.sc[:, :, j], vecs[:, V_C + 16 * j:V_C + 16 * j + 16], AF.Silu, ['vecs'], ['sc'])
        for ci in range(12):
            W, wk = self.wload(l, 'mod', ci)
            for g in range(4):
                cg = ci * 4 + g
                b = self.ps_next()
                for k in range(16):
                    self.mm(self.ps[b][:, 0:2], W[:, k, g * 128:(g + 1) * 128], self.sc[:, k, :],
                            k == 0, k == 15, [wk, 'sc'], ['ps%d' % b])
                self.ts(self.modT[:, cg, :], self.ps[b][:, 0:2], vecs[:, vl + VL_BMOD + cg:vl + VL_BMOD + cg + 1],
                        None, ALU.add, None, ['ps%d' % b, 'vecs'], ['modT'])
        for j in range(2):
            self.ts(self.G[:, :, j], self.modT[:, 16:32, j], 1.0, None, ALU.add, None, ['modT'], ['G'])
            self.tt(self.G[:, :, j], self.G[:, :, j], vecs[:, vl + VL_PREG:vl + VL_PREG + 16], ALU.mult,
                    ['G', 'vecs'], ['G'])
            self.tt(self.GP[:, :, j], self.modT[:, 32:48, j], vecs[:, vl + VL_POSTG:vl + VL_POSTG + 16],
                    ALU.mult, ['modT', 'vecs'], ['GP'])
        dg = self.sb("dg", [128, 4, 128], F32)
        for j in range(2):
            for q4 in range(4):
                b = self.ps_next()
                for i4 in range(4):
                    i = q4 * 4 + i4
                    self.ts(dg[:, i4, :], self.ident_f[:], self.GP[:, i, j:j + 1], None, ALU.mult, None,
                            ['ident_f', 'GP'], ['dg%d' % i4])
                    self.mm(self.ps[b][:, i4 * 128:(i4 + 1) * 128], self.ones_f[:], dg[:, i4, :], True, True,
                            ['ones_f', 'dg%d' % i4], ['ps%d' % b])
                self.P.add('act', lambda e, b=b, j=j, q4=q4: e.copy(out=self.GPb[:, j, q4 * 512:(q4 + 1) * 512],
                                                                   in_=self.ps[b][:, :]),
                           ['ps%d' % b], ['GPb'])
        st = self.sb("wst", [128, 4, 128], F32)
        P.dma('sp', [(st[:, g, :], w['sgu'][g, :, :]) for g in range(4)], [], ['wst'], 'wst')
        wsb = self.sb("wsb", [128, 4, 128], BF16)
        P.add('dve', lambda e: e.tensor_copy(out=wsb[:], in_=st[:]), ['wst'], ['wsb'])
        for g in range(4):
            P.add('pe', lambda e, g=g: e.transpose(self.pst[:, g * 128:(g + 1) * 128], wsb[:, g, :], self.ident_b[:]),
                  ['wsb', 'ident_b'], ['pst'])
        for g in range(4):
            P.add('dve', lambda e, g=g: e.tensor_copy(out=self.WsT[:, g, :], in_=self.pst[:, g * 128:(g + 1) * 128]),
                  ['pst'], ['WsT'])
        P.dma('sp', [(self.lnb[:], self.bct_d[:, l * 1536:(l + 1) * 1536])], [], ['lnb'], 'lnb')

    def prep_attn_weights(self, l):
        P = self.P
        w = self.w[l]
        vl = V_L + l * VL_N
        vecs = self.vecs
        self.Wkv = self.sb("Wkv", [128, 2, 1024], BF16)
        self.Wq = self.sb("Wq", [128, 4, 4, 256], BF16)
        mark = self.off
        st = self.sb("wst2", [128, 4, 1024], F32)
        P.dma('sp', [(st[:, c, :], w['ukv'][c * 128:(c + 1) * 128, :]) for c in range(2)], [], ['wst2'], 'wst2')
        for c in range(2):
            gsc = vecs[:, vl + VL_KVG + c:vl + VL_KVG + c + 1]
            for t in range(2):
                self.ts(self.Wkv[:, c, t * 512:(t + 1) * 512].rearrange("p (h d) -> p h d", h=4),
                        st[:, c, :].rearrange("p (h t d) -> p h t d", h=4, t=2)[:, :, t, :],
                        gsc, None, ALU.mult, None, ['wst2', 'vecs'], ['Wkv'])
        P.dma('sp', [(st[:, c, 0:768], w['uq'][c * 128:(c + 1) * 128, :]) for c in range(4)], [], ['wst2'], 'wst2')
        for c in range(4):
            gsc = vecs[:, vl + VL_QG + c:vl + VL_QG + c + 1]
            s3 = st[:, c, 0:768].rearrange("p (h d) -> p h d", h=4)
            self.ts(self.Wq[:, c, :, 0:192], s3, gsc, None, ALU.mult, None, ['wst2', 'vecs'], ['Wq'])
            self.ts(self.Wq[:, c, :, 192:224], s3[:, :, 160:192], gsc, -1.0, ALU.mult, ALU.mult,
                    ['wst2', 'vecs'], ['Wq'])
            self.ts(self.Wq[:, c, :, 224:256], s3[:, :, 128:160], gsc, None, ALU.mult, None,
                    ['wst2', 'vecs'], ['Wq'])
        P.barrier()
        self.off = mark

    def alloc_hx(self, hxTs=None):
        self.xs = [self.sb("xs%d" % i, [128, D], F32) for i in range(2)]
        self.xn = [self.sb("xn%d" % i, [128, D], BF16) for i in range(2)]
        self.ss = self.sb("ss", [128, 8], F32)
        self.hxTs = hxTs if hxTs is not None else [self.sb("hxT%d" % i, [128, 16, 512], BF16) for i in range(2)]
        self.hcur = 0
        self.hxT = self.hxTs[0]
        self.hk = 'hxT0'
        self.ecnt = 0

    def hx_stage1(self, src, r0, s):
        P = self.P
        b = s % 2
        xs, xn = self.xs[b], self.xn[b]
        rr = r0 + s * 128
        P.dma('sp', [(xs[:], src[rr:rr + 128, :])], ['xsrc'], ['xs%d' % b], 'xs%d' % b)
        ssv = self.ss[:, b:b + 1]
        self.act(xn[:], xs[:], AF.Square, ['xs%d' % b], ['xn%d' % b, 'ss%d' % b], accum_out=ssv)
        self.rsqrt_mean(self.ss[:, 4 + b:5 + b], ssv, D, ['ss%d' % b], ['rr%d' % b])
        self.act(xn[:], xs[:], AF.Copy, ['xs%d' % b, 'rr%d' % b], ['xn%d' % b], scale=self.ss[:, 4 + b:5 + b])

    def hx_stage2(self, hb, s, j):
        P = self.P
        b = s % 2
        xn = self.xn[b]
        hxT = self.hxTs[hb]
        hk = 'hxT%d' % hb
        for i0 in (0, 8):
            self.tbank = 1 - getattr(self, 'tbank', 0)
            if self.tbank:
                tb_, tkey = self.pst, 'pst'
            else:
                tb_, tkey = self.ps[6][:, :].bitcast(BF16), 'ps6'
            for i in range(i0, i0 + 8):
                pt = tb_[:, (i - i0) * 128:(i - i0 + 1) * 128]
                P.add('pe', lambda e, pt=pt, xn=xn, i=i: e.transpose(pt, xn[:, i * 128:(i + 1) * 128], self.ident_b[:]),
                      ['xn%d' % b, 'ident_b'], [tkey])
            for i in range(i0, i0 + 8):
                pt = tb_[:, (i - i0) * 128:(i - i0 + 1) * 128]
                o = hxT[:, i, s * 128:(s + 1) * 128]
                gi = self.G[:, i, j:j + 1]
                si = self.modT[:, i, j:j + 1]
                self.ecnt += 1
                if self.ecnt % 2 == 0:
                    self.act(o, pt, AF.Identity, [tkey, 'G', 'modT'], [hk], bias=si, scale=gi)
                else:
                    self.ts(o, pt, gi, si, ALU.mult, ALU.add, [tkey, 'G', 'modT'], [hk])

    def hx_full(self, src, r0, n, j, hb):
        for s in range(n // 128):
            self.hx_stage1(src, r0, s)
            self.hx_stage2(hb, s, j)

    def hx_pipeline(self, blocks, srcs):
        state = {'p': 0}

        def start_block(bi):
            state['p'] = 0
            self.hcur = bi % 2
            self.hxT = self.hxTs[self.hcur]
            self.hk = 'hxT%d' % self.hcur
            if bi == 0:
                tok0, n, j = blocks[0]
                self.hx_full(srcs[j], 0 if j else tok0, n, j, 0)

        def step(bi):
            p = state['p']
            state['p'] += 1
            if bi + 1 >= len(blocks):
                return
            tok0, n, j = blocks[bi + 1]
            ns = n // 128
            src, r0 = srcs[j], (0 if j else tok0)
            hb = (bi + 1) % 2
            if p < ns:
                self.hx_stage1(src, r0, p)
            if 1 <= p <= ns:
                self.hx_stage2(hb, p - 1, j)

        def finish(bi):
            while state['p'] <= 4:
                step(bi)
        return start_block, step, finish

    def proj_fm(self, W, wk, c0, M, n, b):
        for k in range(16):
            self.mm(self.ps[b][0:M, 0:n], W[:, k, c0:c0 + M], self.hxT[:, k, 0:n], k == 0, k == 15,
                    [wk, self.hk], ['ps%d' % b])

    def cp_act(self, out, in_, reads, writes):
        self.P.add('act', lambda e: e.copy(out=out, in_=in_), reads, writes)

    def cp_dve(self, out, in_, reads, writes):
        self.P.add('dve', lambda e: e.tensor_copy(out=out, in_=in_), reads, writes)

    def phaseA(self, l, blocks, src_x, src_c, last):
        P = self.P
        self.prep_attn_weights(l)
        self.alloc_hx()
        rope = self.sb("rope", [64, 2, 512], F32)
        cfk = self.sb("cfk", [128, 2, 512], BF16)
        sqk = self.sb("sqk", [128, 2, 512], BF16)
        rbk = self.sb("rbk", [128, 512], F32)
        cnk = self.sb("cnk", [128, 2, 512], BF16)
        cfq = self.sb("cfq", [128, 4, 512], BF16)
        sqq = self.sb("sqq", [128, 4, 512], BF16)
        rbq = self.sb("rbq", [128, 512], F32)
        cnq = self.sb("cnq", [128, 4, 512], BF16)
        xcb = self.sb("xcb", [128, 4, 512], BF16)
        kn_st = self.sb("kn_st", [128, 4, 512], BF16)
        kp_st = self.sb("kp_st", [64, 512], BF16)
        v_st = self.sb("v_st", [128, 4, 512], BF16)
        qn_st = self.sb("qn_st", [128, 4, 512], BF16)
        qp_st = self.sb("qp_st", [64, 4, 512], BF16)
        t1 = self.sb("a_t1", [64, 512], F32)
        t2 = self.sb("a_t2", [64, 512], F32)
        ub = self.sb("ub", [128, 4, 512], BF16)
        fa_st = self.sb("fa_st", [128, 4, 512], BF16)
        fb_st = self.sb("fb_st", [128, 4, 512], BF16)
        g_st = self.sb("g_st", [128, 4, 512], BF16)

        def rope_apply(pa, pb, out, okey, n):
            self.tt(t1[:, 0:n], self.ps[pa][0:64, 0:n], rope[:, 0, 0:n], ALU.mult, ['ps%d' % pa, 'rope'], ['a_t1'])
            self.tt(t2[:, 0:n], self.ps[pb][0:64, 0:n], rope[:, 1, 0:n], ALU.mult, ['ps%d' % pb, 'rope'], ['a_t2'])
            self.tt(out, t1[:, 0:n], t2[:, 0:n], ALU.add, ['a_t1', 'a_t2'], [okey])

        def norm_fm(nch, n, dim, cf, sq, rb, cn, tag):
            b = self.ps_next()
            for c in range(nch):
                self.mm(self.ps[b][:, 0:n], self.ones_b[:], sq[:, c, 0:n], c == 0, c == nch - 1,
                        ['ones_b', 'sq' + tag], ['ps%d' % b])
            self.rsqrt_mean(rb[:, 0:n], self.ps[b][:, 0:n], dim, ['ps%d' % b], ['rb' + tag])
            for c in range(nch):
                self.tt(cn[:, c, 0:n], cf[:, c, 0:n], rb[:, 0:n], ALU.mult, ['cf' + tag, 'rb' + tag], ['cn' + tag])

        srcs = {0: src_x, 1: src_c}
        start_block, step, finish = self.hx_pipeline(blocks, srcs)
        for bi, (tok0, n, j) in enumerate(blocks):
            start_block(bi)
            ns = n // 128
            full = not (last and j)
            need_q = not (last and bi >= 4)
            need_g = not (last and bi in (5, 6))
            P.dma('sp', [(rope[:, :, 0:n], self.rope_d[:, :, tok0:tok0 + n])], [], ['rope'], 'rope')
            W, wk = self.wload(l, 'win', 0, [(0, 320, 0), (288, 32, 320), (256, 32, 352)])
            self.ts(W[:, :, 320:352], W[:, :, 320:352], -1.0, None, ALU.mult, None, [wk], [wk])
            for c in range(2):
                b = self.ps_next()
                self.proj_fm(W, wk, c * 128, 128, n, b)
                self.cp_act(cfk[:, c, 0:n], self.ps[b][:, 0:n], ['ps%d' % b], ['cfk'])
                self.act(sqk[:, c, 0:n], self.ps[b][:, 0:n], AF.Square, ['ps%d' % b], ['sqk'])
            pa = self.ps_next()
            self.proj_fm(W, wk, 256, 64, n, pa)
            pb = self.ps_next()
            self.proj_fm(W, wk, 320, 64, n, pb)
            rope_apply(pa, pb, kp_st[:, 0:n], 'kp_st', n)
            P.dma('sp', [(self.KpT[:, tok0:tok0 + n], kp_st[:, 0:n])], ['kp_st'], ['KpT'], 'kp_st')
            norm_fm(2, n, 256, cfk, sqk, rbk, cnk, 'k')
            step(bi)
            if full and need_q:
                W, wk = self.wload(l, 'win', 1)
                for c in range(4):
                    b = self.ps_next()
                    self.proj_fm(W, wk, c * 128, 128, n, b)
                    self.cp_act(cfq[:, c, 0:n], self.ps[b][:, 0:n], ['ps%d' % b], ['cfq'])
                    self.act(sqq[:, c, 0:n], self.ps[b][:, 0:n], AF.Square, ['ps%d' % b], ['sqq'])
                norm_fm(4, n, 512, cfq, sqq, rbq, cnq, 'q')
            step(bi)
            for h in range(4):
                b = self.ps_next()
                for c in range(2):
                    self.mm(self.ps[b][:, 0:n], self.Wkv[:, c, h * 128:(h + 1) * 128], cnk[:, c, 0:n],
                            c == 0, c == 1, ['Wkv', 'cnk'], ['ps%d' % b])
                self.cp_act(kn_st[:, h, 0:n], self.ps[b][:, 0:n], ['ps%d' % b], ['kn_st'])
            P.dma('sp', [(self.KnT.rearrange("(h p) t -> p h t", p=128)[:, :, tok0:tok0 + n], kn_st[:, :, 0:n])],
                  ['kn_st'], ['KnT'], 'kn_st')
            for s in range(ns):
                b = self.ps_next()
                for c in range(2):
                    self.mm(self.ps[b][:, :], cnk[:, c, s * 128:(s + 1) * 128], self.Wkv[:, c, 512:1024],
                            c == 0, c == 1, ['Wkv', 'cnk'], ['ps%d' % b])
                self.cp_dve(v_st[:, s, :], self.ps[b][:, :], ['ps%d' % b], ['v_st'])
            P.dma('sp', [(self.Vd[tok0:tok0 + n, :].rearrange("(s p) d -> p s d", p=128), v_st[:, 0:ns, :])],
                  ['v_st'], ['Vd'], 'v_st')
            if not full:
                finish(bi)
                continue
            W, wk = self.wload(l, 'win', 2 + (COL['uB'] - 832) // 512)
            for c in range(4):
                b = self.ps_next()
                self.proj_fm(W, wk, c * 128, 128, n, b)
                self.cp_act(ub[:, c, 0:n], self.ps[b][:, 0:n], ['ps%d' % b], ['ub'])
            step(bi)
            for h in (range(4) if need_q else ()):
                b = self.ps_next()
                for c in range(4):
                    self.mm(self.ps[b][:, 0:n], self.Wq[:, c, h, 0:128], cnq[:, c, 0:n], c == 0, c == 3,
                            ['Wq', 'cnq'], ['ps%d' % b])
                self.cp_act(qn_st[:, h, 0:n], self.ps[b][:, 0:n], ['ps%d' % b], ['qn_st'])
                pa = self.ps_next()
                for c in range(4):
                    self.mm(self.ps[pa][0:64, 0:n], self.Wq[:, c, h, 128:192], cnq[:, c, 0:n], c == 0, c == 3,
                            ['Wq', 'cnq'], ['ps%d' % pa])
                pb = self.ps_next()
                for c in range(4):
                    self.mm(self.ps[pb][0:64, 0:n], self.Wq[:, c, h, 192:256], cnq[:, c, 0:n], c == 0, c == 3,
                            ['Wq', 'cnq'], ['ps%d' % pb])
                rope_apply(pa, pb, qp_st[:, h, 0:n], 'qp_st', n)
            if need_q:
                P.dma('sp', [(self.QT[0:512, :].rearrange("(h p) t -> p h t", p=128)[:, :, tok0:tok0 + n], qn_st[:, :, 0:n]),
                             (self.QT[512:768, :].rearrange("(h p) t -> p h t", p=64)[:, :, tok0:tok0 + n], qp_st[:, :, 0:n])],
                      ['qn_st', 'qp_st'], ['QT'], 'q_st')
            if need_g:
                W, wk = self.wload(l, 'win', 2 + (COL['xC'] - 832) // 512)
                for c in range(4):
                    b = self.ps_next()
                    self.proj_fm(W, wk, c * 128, 128, n, b)
                    self.cp_act(xcb[:, c, 0:n], self.ps[b][:, 0:n], ['ps%d' % b], ['xcb'])
            step(bi)
            for s in range(ns):
                for (tsel, stt_, skey) in ((0, fa_st, 'fa_st'), (1, fb_st, 'fb_st')):
                    b = self.ps_next()
                    for g in range(4):
                        self.mm(self.ps[b][:, g * 128:(g + 1) * 128], ub[:, g, s * 128:(s + 1) * 128],
                                self.c128[:, tsel * 128:(tsel + 1) * 128], True, True, ['ub', 'c128'], ['ps%d' % b])
                    self.cp_dve(stt_[:, s, :], self.ps[b][:, :], ['ps%d' % b], [skey])
            P.dma('sp', [(self.FA[tok0:tok0 + n, :].rearrange("(s p) d -> p s d", p=128), fa_st[:, 0:ns, :])],
                  ['fa_st'], ['FA'], 'fa_st')
            P.dma('sp', [(self.FB[tok0:tok0 + n, :].rearrange("(s p) d -> p s d", p=128), fb_st[:, 0:ns, :])],
                  ['fb_st'], ['FB'], 'fb_st')
            if need_g:
                W, wk = self.wload(l, 'win', 2 + (COL['cC'] - 832) // 512)
                for c in range(4):
                    b = self.ps_next()
                    self.proj_fm(W, wk, c * 128, 128, n, b)
                    self.tt(g_st[:, c, 0:n], self.ps[b][:, 0:n], xcb[:, c, 0:n], ALU.mult, ['ps%d' % b, 'xcb'], ['g_st'])
                P.dma('sp', [(self.GT.rearrange("(c p) t -> p c t", p=128)[:, :, tok0:tok0 + n], g_st[:, :, 0:n])],
                      ['g_st'], ['GT'], 'g_st')
            finish(bi)

    def phaseF(self, l, blocks):
        P = self.P
        fa = self.sb("fa", [128, 34, 512], BF16)
        fb = self.sb("fb", [128, 34, 512], BF16)
        tb = [self.sb("tb%d" % i, [128, 2, 4, 512], BF16) for i in range(4)]
        yb_st = [self.sb("yb_st%d" % i, [128, 4, 512], BF16) for i in range(2)]
        P.dma('sp', [(fa[:, q * 17:(q + 1) * 17, :], self.FA[q * 17 * 128:(q + 1) * 17 * 128, :].rearrange("(s p) d -> p s d", p=128))
                     for q in range(2)], ['FA'], ['fa'], 'fa')
        P.dma('sp', [(fb[:, q * 17:(q + 1) * 17, :], self.FB[q * 17 * 128:(q + 1) * 17 * 128, :].rearrange("(s p) d -> p s d", p=128))
                     for q in range(2)], ['FB'], ['fb'], 'fb')
        tbn = 0
        acc = (0, 1, 2, 3)
        for bi, (tok0, n, j) in enumerate(blocks):
            ys = yb_st[bi % 2]
            yk = 'yb_st%d' % (bi % 2)
            if not j:
                for jg in range(8):
                    t = tbn % 4
                    tbn += 1
                    P.dma('sp', [(tb[t][:, 0, :, :], self.TC[jg * 512:(jg + 1) * 512, tok0:tok0 + 512].rearrange("(q p) n -> p q n", p=128)),
                                 (tb[t][:, 1, :, :], self.TS[jg * 512:(jg + 1) * 512, tok0:tok0 + 512].rearrange("(q p) n -> p q n", p=128))],
                          ['cv_T'], ['tb%d' % t], 'tb%d' % t)
                    for q in range(4):
                        jj = jg * 4 + q
                        for c in range(4):
                            self.mm(self.ps[acc[c]][:, :], fa[:, jj, c * 128:(c + 1) * 128], tb[t][:, 0, q, :],
                                    jj == 0, False, ['fa', 'tb%d' % t], ['ps%d' % acc[c]])
                            self.mm(self.ps[acc[c]][:, :], fb[:, jj, c * 128:(c + 1) * 128], tb[t][:, 1, q, :],
                                    False, jj == 31, ['fb', 'tb%d' % t], ['ps%d' % acc[c]])
            else:
                for q in range(2):
                    jj = 32 + q
                    for c in range(4):
                        self.mm(self.ps[acc[c]][:, 0:n], fa[:, jj, c * 128:(c + 1) * 128], self.tcx[:, q, 0, :],
                                q == 0, False, ['fa', 'tcx'], ['ps%d' % acc[c]])
                        self.mm(self.ps[acc[c]][:, 0:n], fb[:, jj, c * 128:(c + 1) * 128], self.tcx[:, q, 1, :],
                                False, q == 1, ['fb', 'tcx'], ['ps%d' % acc[c]])
            for c in range(4):
                if c % 2 == 0:
                    self.cp_act(ys[:, c, 0:n], self.ps[acc[c]][:, 0:n], ['ps%d' % acc[c]], [yk])
                else:
                    self.cp_dve(ys[:, c, 0:n], self.ps[acc[c]][:, 0:n], ['ps%d' % acc[c]], [yk])
            P.dma('sp', [(self.YB.rearrange("(c p) t -> p c t", p=128)[:, :, tok0:tok0 + n], ys[:, :, 0:n])],
                  [yk], ['YB'], yk)

    def phaseQ(self, l, blocks):
        P = self.P
        kn = self.sb("kn", [128, 4, NT], BF16)
        kp = self.sb("kp", [64, NT], BF16)
        v = self.sb("v", [128, 34, 512], BF16)
        qn = [self.sb("qn%d" % i, [128, 4, 512], BF16) for i in range(2)]
        qp = [self.sb("qp%d" % i, [64, 4, 512], BF16) for i in range(2)]
        pt = [self.sb("pt%d" % i, [128, 512], BF16) for i in range(4)]
        rl = self.sb("rl", [128, 512], F32)
        at_st = [self.sb("at_st%d" % i, [128, 4, 512], BF16) for i in range(2)]
        P.dma('sp', [(kn[:, h, :], self.KnT[h * 128:(h + 1) * 128, :]) for h in range(4)], ['KnT'], ['kn'], 'kn')
        P.dma('sp', [(kp[:], self.KpT)], ['KpT'], ['kp'], 'kp')
        P.dma('sp', [(v[:, q * 17:(q + 1) * 17, :], self.Vd[q * 17 * 128:(q + 1) * 17 * 128, :].rearrange("(s p) d -> p s d", p=128))
                     for q in range(2)], ['Vd'], ['v'], 'v')
        LOOK = 2
        seq = []
        for bi, (tok0, n, j) in enumerate(blocks):
            kts = list(range(34)) if not j else [32, 33]
            for h in range(4):
                for ki, kt in enumerate(kts):
                    seq.append((bi, h, ki, kt, len(kts)))
        ptb = {}
        grp = {}
        psum4 = [self.sb("psum4_%d" % i, [128, 512], BF16) for i in range(2)]

        def stage_s(idx):
            bi, h, ki, kt, nk = seq[idx]
            tok0, n, j = blocks[bi]
            qb = bi % 2
            if h == 0 and ki == 0:
                P.dma('sp', [(qn[qb][:, :, 0:n], self.QT[0:512, :].rearrange("(h p) t -> p h t", p=128)[:, :, tok0:tok0 + n]),
                             (qp[qb][:, :, 0:n], self.QT[512:768, :].rearrange("(h p) t -> p h t", p=64)[:, :, tok0:tok0 + n])],
                      ['QT'], ['q%d' % qb], 'q%d' % qb)
            sbk = self.ps_next(pool=(0, 1, 2))
            self.mm(self.ps[sbk][:, 0:n], kn[:, h, kt * 128:(kt + 1) * 128], qn[qb][:, h, 0:n], True, False,
                    ['kn', 'q%d' % qb], ['ps%d' % sbk])
            self.mm(self.ps[sbk][:, 0:n], kp[:, kt * 128:(kt + 1) * 128], qp[qb][:, h, 0:n], False, True,
                    ['kp', 'q%d' % qb], ['ps%d' % sbk])
            p_ = idx % 4
            ptb[idx] = p_
            self.act(pt[p_][:, 0:n], self.ps[sbk][:, 0:n], AF.Exp, ['ps%d' % sbk], ['pt%d' % p_], scale=SCALE)

        def stage_pv(idx):
            bi, h, ki, kt, nk = seq[idx]
            tok0, n, j = blocks[bi]
            qb = bi % 2
            ats = at_st[qb]
            O, Lb = (3, 4) if (bi * 4 + h) % 2 == 0 else (5, 6)
            p_ = ptb.pop(idx)
            self.mm(self.ps[O][:, 0:n], v[:, kt, h * 128:(h + 1) * 128], pt[p_][:, 0:n], ki == 0, ki == nk - 1,
                    ['v', 'pt%d' % p_], ['ps%d' % O])
            self.mm(self.ps[Lb][:, 0:n], self.ones_b[:], pt[p_][:, 0:n], ki == 0, ki == nk - 1,
                    ['ones_b', 'pt%d' % p_], ['ps%d' % Lb])
            if ki == nk - 1:
                self.P.add('dve', lambda e, n=n, Lb=Lb: e.reciprocal(out=rl[:, 0:n], in_=self.ps[Lb][:, 0:n]), ['ps%d' % Lb], ['rl'])
                self.tt(ats[:, h, 0:n], self.ps[O][:, 0:n], rl[:, 0:n], ALU.mult, ['ps%d' % O, 'rl'], ['at_st%d' % qb])
                if h == 3:
                    P.dma('sp', [(self.AT.rearrange("(h p) t -> p h t", p=128)[:, :, tok0:tok0 + n], ats[:, :, 0:n])],
                          ['at_st%d' % qb], ['AT'], 'at_st%d' % qb)

        for idx in range(len(seq) + LOOK):
            if idx < len(seq):
                stage_s(idx)
            if idx - LOOK >= 0:
                stage_pv(idx - LOOK)

    def phaseB(self, l, blocks, src_x, src_c, last):
        P = self.P
        w = self.w[l]
        vl = V_L + l * VL_N
        vecs = self.vecs
        ureg = self.sb("ureg", [128, 24576], BF16)
        u0 = self.last_off
        osbs = [self.sb("osb%d" % i, [128, 4, D], F32, at=u0 + 16384 * i) for i in range(2)]
        hxTs = [self.sb("hxT%d" % i, [128, 16, 512], BF16, at=u0 + 32768 * i) for i in range(2)]
        ys = [self.sb("ys%d" % g, [128, 4, 512], BF16, at=u0 + 16384 + g * 4096) for g in range(4)]
        self.alloc_hx(hxTs=hxTs)
        srcs = {0: src_x, 1: src_c}
        start_block, step, finish = self.hx_pipeline(blocks, srcs)
        Wb = [self.sb("Wb%d" % i, [128, 4, 512], BF16) for i in range(2)]
        yb = self.sb("yb", [128, 4, 512], BF16)
        at = self.sb("at", [128, 4, 512], BF16)
        gt = self.sb("gt", [128, 4, 514], BF16)
        sg = [self.sb("sg%d" % i, [128, 512], F32) for i in range(3)]
        tmp = [self.sb("btmp%d" % i, [128, 512], F32) for i in range(3)]
        acc = [self.sb("bacc%d" % i, [128, 512], F32) for i in range(4)]
        mT = self.sb("mT", [128, 16, 512], BF16)
        vc = self.sb("vc", [128, 4, 512], BF16)
        vf = self.sb("vf", [128, 512], F32)
        mix = self.sb("mix", [128, 4, 512], F32)
        st6 = self.sb("st6", [128, 8], F32)
        oss = self.sb("oss", [128, 24], F32)
        sgn = 0
        tmn = 0
        wbn = 0
        GTv = self.GT.rearrange("(c p) t -> p c t", p=128)
        for bi, (tok0, n, j) in enumerate(blocks):
            src = src_c if j else src_x
            r0 = 0 if j else tok0
            ns = n // 128
            start_block(bi)
            osb = osbs[bi % 2]
            UK = ['hxT%d' % (bi % 2), 'ys0', 'ys1', 'ys2', 'ys3']
            P.dma('sp', [(yb[:, :, 0:n], self.YB.rearrange("(c p) t -> p c t", p=128)[:, :, tok0:tok0 + n])], ['YB'], ['yb'], 'yb')
            P.dma('sp', [(at[:, :, 0:n], self.AT.rearrange("(c p) t -> p c t", p=128)[:, :, tok0:tok0 + n])], ['AT'], ['at'], 'at')
            if j:
                lidx, ridx, lm, rm = SEQ, SEQ, 3, 3
            else:
                k = tok0 // 512
                lidx = (tok0 - 1) % SEQ
                ridx = (tok0 + 512) % SEQ
                lm = 0 if k == 0 else (1 if k == 4 else 2)
                rm = 1 if k == 3 else (0 if k == 7 else 2)
            P.dma('sp', [(gt[:, :, 1:n + 1], GTv[:, :, tok0:tok0 + n])], ['GT'], ['gt'], 'gt')
            P.dma('sp', [(gt[:, :, 0:1], GTv[:, :, lidx:lidx + 1]),
                         (gt[:, :, n + 1:n + 2], GTv[:, :, ridx:ridx + 1])], ['GT'], ['gt'], 'gt',
                  allow_slow_non_contiguous=True)
            self.ts(gt[:, :, 0:1], gt[:, :, 0:1], vecs[:, V_MASK + lm:V_MASK + lm + 1], None, ALU.mult, None, ['gt', 'vecs'], ['gt'])
            self.ts(gt[:, :, n + 1:n + 2], gt[:, :, n + 1:n + 2], vecs[:, V_MASK + rm:V_MASK + rm + 1], None, ALU.mult, None,
                    ['gt', 'vecs'], ['gt'])

            def wcol(name):
                return self.wload(l, 'win', 2 + (COL[name] - 832) // 512)

            def silu_chunk(W, wk, c):
                nonlocal sgn
                b = self.ps_next()
                self.proj_fm(W, wk, c * 128, 128, n, b)
                s_ = sgn % 3
                sgn += 1
                self.act(sg[s_][:, 0:n], self.ps[b][:, 0:n], AF.Silu, ['ps%d' % b], ['sg%d' % s_])
                return s_

            for (zname, other, okey, g) in (('zA', at, 'at', 0), ('zB', yb, 'yb', 1)):
                W, wk = wcol(zname)
                for c in range(4):
                    s_ = silu_chunk(W, wk, c)
                    self.tt(ys[g][:, c, 0:n], sg[s_][:, 0:n], other[:, c, 0:n], ALU.mult, ['sg%d' % s_, okey], ['ys%d' % g])
            step(bi)
            Wz, wkz = wcol('zC')
            Wc, wkc = wcol('bC')
            for c in range(4):
                s_ = silu_chunk(Wz, wkz, c)
                t_ = tmn % 3
                tmn += 1
                T = tmp[t_]
                tk = 'btmp%d' % t_
                cw = [vecs[:, vl + VL_CW + k_ * 4 + c:vl + VL_CW + k_ * 4 + c + 1] for k_ in range(3)]
                self.ts(T[:, 0:n], gt[:, c, 0:n], cw[0], None, ALU.mult, None, ['gt', 'vecs'], [tk])
                self.stt(T[:, 0:n], gt[:, c, 1:n + 1], cw[1], T[:, 0:n], ALU.mult, ALU.add, ['gt', 'vecs', tk], [tk])
                self.stt(T[:, 0:n], gt[:, c, 2:n + 2], cw[2], T[:, 0:n], ALU.mult, ALU.add, ['gt', 'vecs', tk], [tk])
                b2 = self.ps_next()
                self.proj_fm(Wc, wkc, c * 128, 128, n, b2)
                self.stt(T[:, 0:n], T[:, 0:n], vecs[:, vl + VL_CB + c:vl + VL_CB + c + 1], self.ps[b2][:, 0:n], ALU.add, ALU.mult,
                         [tk, 'vecs', 'ps%d' % b2], [tk])
                self.tt(ys[2][:, c, 0:n], T[:, 0:n], sg[s_][:, 0:n], ALU.mult, [tk, 'sg%d' % s_], ['ys2'])
            step(bi)
            Wv, wkv = wcol('vD')
            for s in range(ns):
                b = self.ps_next()
                for k in range(16):
                    self.mm(self.ps[b][:, :], self.hxT[:, k, s * 128:(s + 1) * 128], Wv[:, k, 0:512], k == 0, k == 15,
                            [wkv, self.hk], ['ps%d' % b])
                self.act(vf[:, :], self.ps[b][:, :], AF.Identity, ['ps%d' % b], ['vf', 'st6a'], accum_out=st6[:, 0:1])
                self.act(self.jk[:, :], self.ps[b][:, :], AF.Square, ['ps%d' % b], ['jk', 'st6b'], accum_out=st6[:, 1:2])
                self.ts(st6[:, 2:3], st6[:, 0:1], 1.0 / 512, None, ALU.mult, None, ['st6a'], ['st6c'])
                self.tt(st6[:, 3:4], st6[:, 2:3], st6[:, 2:3], ALU.mult, ['st6c'], ['st6d'])
                self.stt(st6[:, 4:5], st6[:, 1:2], 1.0 / 512, st6[:, 3:4], ALU.mult, ALU.subtract, ['st6b', 'st6d'], ['st6e'])
                self.ts(st6[:, 5:6], st6[:, 4:5], EPS, None, ALU.add, None, ['st6e'], ['st6f'])
                self.act(st6[:, 5:6], st6[:, 5:6], AF.Sqrt, ['st6f'], ['st6f'])
                self.P.add('dve', lambda e: e.reciprocal(out=st6[:, 5:6], in_=st6[:, 5:6]), ['st6f'], ['st6f'])
                self.ts(vf[:, :], vf[:, :], st6[:, 2:3], st6[:, 5:6], ALU.subtract, ALU.mult, ['vf', 'st6c', 'st6f'], ['vf'])
                self.tt(vf[:, :], vf[:, :], self.lnb[:, 0:512], ALU.mult, ['vf', 'lnb'], ['vf'])
                self.tt(vc[:, s, :], vf[:, :], self.lnb[:, 512:1024], ALU.add, ['vf', 'lnb'], ['vc'])
            for g in range(4):
                b = self.ps_next()
                for s in range(ns):
                    self.mm(self.ps[b][:, s * 128:(s + 1) * 128], vc[:, s, g * 128:(g + 1) * 128], self.WsT[:, g, :], True, True,
                            ['vc', 'WsT'], ['ps%d' % b])
                for s in range(ns):
                    self.tt(mix[:, g, s * 128:(s + 1) * 128], self.ps[b][:, s * 128:(s + 1) * 128],
                            self.lnb[:, 1024 + g * 128:1024 + (g + 1) * 128], ALU.add, ['ps%d' % b, 'lnb'], ['mix'])
            Wu, wku = wcol('uD')
            for c in range(4):
                b = self.ps_next()
                self.proj_fm(Wu, wku, c * 128, 128, n, b)
                self.tt(mix[:, c, 0:n], self.ps[b][:, 0:n], mix[:, c, 0:n], ALU.mult, ['ps%d' % b, 'mix'], ['mix'])
            Wz, wkz = wcol('zD')
            for c in range(4):
                s_ = silu_chunk(Wz, wkz, c)
                self.tt(ys[3][:, c, 0:n], mix[:, c, 0:n], sg[s_][:, 0:n], ALU.mult, ['mix', 'sg%d' % s_], ['ys3'])
            step(bi)
            for jd in range(4):
                if jd in (1, 2):
                    step(bi)
                for g in range(4):
                    W, wk = self.wload(l, 'win', 12 + 4 * g + jd)
                    wb = wbn % 2
                    wbn += 1
                    if self.conv.get((l, 'br')):
                        P.dma('pool', [(Wb[wb][:], self.CW[l]['br'][g * 4 + jd])], ['cv_br%d' % l], ['Wb%d' % wb], 'Wb%d' % wb)
                    else:
                        P.dma('pool', [(Wb[wb][:], w['br'][g, :, jd * 512:(jd + 1) * 512].rearrange("(k p) c -> p k c", p=128))],
                              [], ['Wb%d' % wb], 'Wb%d' % wb)
                    for dc in range(4):
                        b = self.ps_next()
                        self.proj_fm(W, wk, dc * 128, 128, n, b)
                        s_ = sgn % 3
                        sgn += 1
                        self.act(sg[s_][:, 0:n], self.ps[b][:, 0:n], AF.Sigmoid, ['ps%d' % b], ['sg%d' % s_])
                        b2 = self.ps_next()
                        for k in range(4):
                            self.mm(self.ps[b2][:, 0:n], Wb[wb][:, k, dc * 128:(dc + 1) * 128], ys[g][:, k, 0:n], k == 0, k == 3,
                                    ['Wb%d' % wb, 'ys%d' % g], ['ps%d' % b2])
                        A_ = acc[dc]
                        ak = 'bacc%d' % dc
                        sk = 'sg%d' % s_
                        if g == 0:
                            self.tt(A_[:, 0:n], sg[s_][:, 0:n], self.ps[b2][:, 0:n], ALU.mult, [sk, 'ps%d' % b2], [ak])
                        else:
                            self.tt(sg[s_][:, 0:n], sg[s_][:, 0:n], self.ps[b2][:, 0:n], ALU.mult, [sk, 'ps%d' % b2], [sk])
                            if g < 3:
                                self.tt(A_[:, 0:n], A_[:, 0:n], sg[s_][:, 0:n], ALU.add, [ak, sk], [ak])
                            else:
                                self.tt(mT[:, jd * 4 + dc, 0:n], A_[:, 0:n], sg[s_][:, 0:n], ALU.add, [ak, sk], ['mT'])
            finish(bi)
            for jo in range(4):
                W, wk = self.wload(l, 'out', jo)
                for s in range(ns):
                    b = self.ps_next()
                    for k in range(16):
                        self.mm(self.ps[b][:, :], mT[:, k, s * 128:(s + 1) * 128], W[:, k, 0:512], k == 0, k == 15,
                                [wk, 'mT'], ['ps%d' % b])
                    self.cp_act(osb[:, s, jo * 512:(jo + 1) * 512], self.ps[b][:, :], ['ps%d' % b], ['osb%d' % s] + UK)
                    self.act(self.jk[:, :], self.ps[b][:, :], AF.Square, ['ps%d' % b], ['jk', 'oss%d_%d' % (s, jo)],
                             accum_out=oss[:, s * 4 + jo:s * 4 + jo + 1])
            for s in range(ns):
                self.P.add('dve', lambda e, s=s: e.reduce_sum(out=oss[:, 16 + s:17 + s], in_=oss[:, s * 4:s * 4 + 4],
                                                              axis=mybir.AxisListType.X),
                           ['oss%d_%d' % (s, q) for q in range(4)], ['orr%d' % s])
                self.rsqrt_mean(oss[:, 20 + s:21 + s], oss[:, 16 + s:17 + s], D, ['orr%d' % s], ['orq%d' % s])
                xb = s % 2
                xrb = self.xs[xb]
                xk = 'xs%d' % xb
                rr0 = r0 + s * 128
                P.dma('sp', [(xrb[:], src[rr0:rr0 + 128, :])], ['xsrc'], [xk], xk)
                self.stt(osb[:, s, :], osb[:, s, :], oss[:, 20 + s:21 + s], self.GPb[:, j, :], ALU.mult, ALU.mult,
                         ['osb%d' % s, 'orq%d' % s, 'GPb'] + UK, ['osb%d' % s] + UK)
                self.tt(xrb[:], xrb[:], osb[:, s, :], ALU.add, [xk, 'osb%d' % s] + UK, [xk])
                if j:
                    dst, dk = self.C1[rr0:rr0 + 128, :], 'xdst'
                elif last:
                    dst, dk = self.y[rr0:rr0 + 128, :], 'xdst'
                else:
                    dst, dk = self.X1[rr0:rr0 + 128, :], 'xdst'
                P.dma('sp', [(dst, xrb[:])], [xk], [dk], xk)


def _fm(v, n):
    return np.ascontiguousarray(np.asarray(v, np.float32).reshape(n, 128).T)


def prepare_core_inputs(inputs, core, layers=(0, 1)):
    b, r = core // 2, core % 2
    f32 = np.float32
    perm = (np.arange(SEQ) + HALF * r) % SEQ
    d = {}
    d['x'] = np.ascontiguousarray(np.asarray(inputs['x'])[b][perm])
    d['ctx'] = np.ascontiguousarray(np.asarray(inputs['ctx'])[b])
    vecs = np.zeros((128, NV), f32)
    vecs[:, V_C:V_C + 16] = _fm(inputs['c'][b], 16)
    vecs[:, V_C + 16:V_C + 32] = _fm(inputs['c_ctx'], 16)
    for l in range(L):
        o = V_L + l * VL_N
        vecs[:, o + VL_BMOD:o + VL_BMOD + 48] = _fm(inputs['b_mod'][l], 48)
        vecs[:, o + VL_PREG:o + VL_PREG + 16] = _fm(inputs['pre_g'][l], 16)
        vecs[:, o + VL_POSTG:o + VL_POSTG + 16] = _fm(inputs['post_g'][l], 16)
        vecs[:, o + VL_QG:o + VL_QG + 4] = _fm(inputs['q_norm_g'][l], 4)
        vecs[:, o + VL_KVG:o + VL_KVG + 2] = _fm(inputs['kv_norm_g'][l], 2)
        for k in range(3):
            vecs[:, o + VL_CW + 4 * k:o + VL_CW + 4 * k + 4] = _fm(inputs['conv_w'][l][k], 4)
        vecs[:, o + VL_CB:o + VL_CB + 4] = _fm(inputs['conv_b'][l], 4)
    vecs[:, V_MASK + 0] = 1.0 if r == 1 else 0.0
    vecs[:, V_MASK + 1] = 1.0 if r == 0 else 0.0
    vecs[:, V_MASK + 2] = 1.0
    vecs[:, V_MASK + 3] = 0.0
    d['vecs'] = vecs
    bct = np.zeros((128, NB), f32)
    for l in range(L):
        bct[:, l * 1536:l * 1536 + 512] = np.asarray(inputs['sgu_ln_g'][l], f32)[None, :]
        bct[:, l * 1536 + 512:l * 1536 + 1024] = np.asarray(inputs['sgu_ln_b'][l], f32)[None, :]
        bct[:, l * 1536 + 1024:l * 1536 + 1536] = np.asarray(inputs['sgu_b'][l], f32).reshape(1, 512)
    d['bct'] = bct
    d['ident'] = np.eye(128, dtype=f32)
    nat = perm.astype(np.float64)
    rows, cols = np.floor(nat / 64), nat % 64
    inv = 10000.0 ** (-np.arange(0, 32, 2, dtype=np.float64) / 32)
    ang = np.concatenate([rows[:, None] * inv, cols[:, None] * inv], -1)
    ang = (np.concatenate([rows[:, None].astype(f32) * inv.astype(f32), cols[:, None].astype(f32) * inv.astype(f32)], -1)).astype(f32)
    cos, sin = np.cos(ang).astype(f32), np.sin(ang).astype(f32)
    rope = np.zeros((64, 2, NT), f32)
    rope[0:32, 0, :SEQ] = cos.T
    rope[32:64, 0, :SEQ] = cos.T
    rope[0:32, 1, :SEQ] = sin.T
    rope[32:64, 1, :SEQ] = sin.T
    rope[:, 0, SEQ:] = 1.0
    d['ropeT'] = rope
    cc = np.arange(128, dtype=np.float64)
    a128 = 2 * np.pi * np.outer(cc, cc) / 128
    d['c128s'] = np.concatenate([np.cos(a128), np.sin(a128)], 1).astype(f32) / np.sqrt(128).astype(f32)
    kk = (perm[:, None].astype(np.int64) * perm[None, :].astype(np.int64)) % SEQ
    a4 = 2 * np.pi * np.arange(SEQ, dtype=np.float64) / SEQ
    d['tcf'] = (np.cos(a4) / 64.0).astype(f32)[kk]
    d['tsf'] = (-np.sin(a4) / 64.0).astype(f32)[kk]
    n256 = np.arange(CTX, dtype=np.float64)
    ac = 2 * np.pi * np.outer(n256, n256) / CTX
    tcx = np.zeros((CTX, 2, CTX), f32)
    tcx[:, 0, :] = np.cos(ac) / 16.0
    tcx[:, 1, :] = -np.sin(ac) / 16.0
    d['tcx'] = tcx
    for l in layers:
        d['w_mod%d' % l] = np.asarray(inputs['w_mod'][l], f32)
        d['w_in%d' % l] = np.asarray(inputs['w_in'][l], f32)
        d['w_uq%d' % l] = np.asarray(inputs['w_uq'][l], f32)
        d['w_ukv%d' % l] = np.asarray(inputs['w_ukv'][l], f32)
        d['sgu_w%d' % l] = np.asarray(inputs['sgu_w'][l], f32)
        d['w_branch%d' % l] = np.asarray(inputs['w_branch'][l], f32)
        d['w_out%d' % l] = np.asarray(inputs['w_out'][l], f32)
    return d


def kernel(**inputs):
    inputs = {k: np.asarray(v) for k, v in inputs.items()}
    nc = Builder().build()
    in_maps = [prepare_core_inputs(inputs, c) for c in range(8)]
    res = run_bass_kernel_spmd(nc, in_maps, core_ids=list(range(8)))
    out = np.zeros((4, SEQ, D), np.float32)
    for c in range(8):
        b, r = c // 2, c % 2
        out[b, HALF * r:HALF * (r + 1), :] = res.results[c]["y"]
    return out
```
